# Optimizing a Trainium2 kernel written in Bass

```python
import jax, jax.numpy as jnp
from jax import lax
import numpy as np

D_MODEL = 1024
BATCH = 2
SEQ = 8192
DEPTH = 4

GRID_W = 64
CTX_LEN = 256
N_Q_HEADS = 8
N_KV_HEADS = 2
HEAD_DIM = 64
AXIS_DIM = HEAD_DIM // 2
ROPE_THETA = 10000.0
Q_BLOCK = 128
CONV_DIM = 512
CONV_WIDTH = 31
N_FOURIER_GROUPS = 4
FOURIER_GROUP = D_MODEL // N_FOURIER_GROUPS
N_EXPERTS = 16
EXPERT_FF = 2816
CAPACITY_FACTOR = 2
ATTN_DIM = N_Q_HEADS * HEAD_DIM
KV_DIM = N_KV_HEADS * HEAD_DIM
IN_DIM = ATTN_DIM + 2 * KV_DIM + 2 * CONV_DIM
MIX_DIM = ATTN_DIM + CONV_DIM
N_EVEN = (DEPTH + 1) // 2
N_ODD = DEPTH // 2
DEEPNORM_ALPHA = (2 * DEPTH) ** 0.25
DEEPNORM_BETA = (8 * DEPTH) ** -0.25
NORM_EPS = 1e-6

kernel_name = 'hybrid_attn_conformer_fnet_ecmoe_diffusion'


def layer_norm(x, g, b):
    xf = x.astype(jnp.float32)
    mu = jnp.mean(xf, axis=-1, keepdims=True)
    var = jnp.mean(jnp.square(xf - mu), axis=-1, keepdims=True)
    return ((xf - mu) * lax.rsqrt(var + NORM_EPS) * g.astype(jnp.float32) + b.astype(jnp.float32)).astype(x.dtype)


def rms_norm(x, g):
    xf = x.astype(jnp.float32)
    return (xf * lax.rsqrt(jnp.mean(xf * xf, axis=-1, keepdims=True) + NORM_EPS) * g.astype(jnp.float32)).astype(x.dtype)


def axial_rope_tables(n_tok):
    rows = n_tok // GRID_W
    row = jnp.repeat(jnp.arange(rows, dtype=jnp.float32), GRID_W)
    col = jnp.tile(jnp.arange(GRID_W, dtype=jnp.float32), rows)
    inv_freq = ROPE_THETA ** (-jnp.arange(0, AXIS_DIM, 2, dtype=jnp.float32) / AXIS_DIM)
    ang = jnp.concatenate([row[:, None] * inv_freq, col[:, None] * inv_freq], axis=-1)
    return jnp.cos(ang), jnp.sin(ang)


def apply_rope(x, cos, sin):
    xf = x.astype(jnp.float32).reshape(x.shape[:-1] + (HEAD_DIM // 2, 2))
    x0, x1 = xf[..., 0], xf[..., 1]
    c = cos[None, :, None, :]
    s = sin[None, :, None, :]
    out = jnp.stack([x0 * c - x1 * s, x0 * s + x1 * c], axis=-1)
    return out.reshape(x.shape).astype(x.dtype)


def attend(q, k, v):
    s = jnp.einsum('bqkgd,bskd->bkgqs', q, k).astype(jnp.float32) * (HEAD_DIM ** -0.5)
    p = jax.nn.softmax(s, axis=-1).astype(v.dtype)
    return jnp.einsum('bkgqs,bskd->bqkgd', p, v)


def gqa_attention(q_lat, k_lat, v_lat, q_ctx, k_ctx, v_ctx, need_ctx):
    b, n = q_lat.shape[:2]
    grp = N_Q_HEADS // N_KV_HEADS
    k_all = jnp.concatenate([k_ctx, k_lat], axis=1)
    v_all = jnp.concatenate([v_ctx, v_lat], axis=1)
    n_blk = n // Q_BLOCK
    qb = q_lat.reshape(b, n_blk, Q_BLOCK, N_KV_HEADS, grp, HEAD_DIM).transpose(1, 0, 2, 3, 4, 5)
    o = lax.map(lambda qq: attend(qq, k_all, v_all), qb)
    y_lat = o.transpose(1, 0, 2, 3, 4, 5).reshape(b, n, ATTN_DIM)
    y_ctx = None
    if need_ctx:
        nc = q_ctx.shape[1]
        qc = q_ctx.reshape(b, nc, N_KV_HEADS, grp, HEAD_DIM)
        y_ctx = attend(qc, k_ctx, v_ctx).reshape(b, nc, ATTN_DIM)
    return y_lat, y_ctx


def conformer_conv(a, g, w, bias, ln_g, ln_b):
    u = a * jax.nn.sigmoid(g)
    pad = CONV_WIDTH // 2
    u = lax.conv_general_dilated(u, w[:, None, :].astype(u.dtype), window_strides=(1,), padding=[(pad, pad)],
                                 dimension_numbers=('NWC', 'WIO', 'NWC'), feature_group_count=CONV_DIM) + bias
    return jax.nn.silu(layer_norm(u, ln_g, ln_b))


def mixer_attn_conv(h_lat, h_ctx, cos, sin, w_in, b_in, qn_g, kn_g, conv_w, conv_b, cln_g, cln_b, w_out, b_out, need_ctx):
    splits = [ATTN_DIM, ATTN_DIM + KV_DIM, ATTN_DIM + 2 * KV_DIM, ATTN_DIM + 2 * KV_DIM + CONV_DIM]

    def project(h):
        b, n = h.shape[:2]
        p = h @ w_in + b_in
        q, k, v, a, g = jnp.split(p, splits, axis=-1)
        q = rms_norm(q.reshape(b, n, N_Q_HEADS, HEAD_DIM), qn_g)
        k = rms_norm(k.reshape(b, n, N_KV_HEADS, HEAD_DIM), kn_g)
        v = v.reshape(b, n, N_KV_HEADS, HEAD_DIM)
        return q, k, v, a, g

    ql, kl, vl, al, gl = project(h_lat)
    ql = apply_rope(ql, cos, sin)
    kl = apply_rope(kl, cos, sin)
    qc, kc, vc, ac, gc = project(h_ctx)
    att_l, att_c = gqa_attention(ql, kl, vl, qc, kc, vc, need_ctx)
    conv_l = conformer_conv(al, gl, conv_w, conv_b, cln_g, cln_b)
    y_lat = jnp.concatenate([att_l, conv_l], axis=-1) @ w_out + b_out
    y_ctx = None
    if need_ctx:
        conv_c = conformer_conv(ac, gc, conv_w, conv_b, cln_g, cln_b)
        y_ctx = jnp.concatenate([att_c, conv_c], axis=-1) @ w_out + b_out
    return y_lat, y_ctx


def fourier_mix(h):
    b, n, _ = h.shape
    hg = h.astype(jnp.float32).reshape(b, n, N_FOURIER_GROUPS, FOURIER_GROUP)
    f = jnp.fft.fft2(hg, axes=(1, 3), norm='ortho').real
    return f.reshape(b, n, D_MODEL).astype(h.dtype)


def expert_choice_moe(h, w_r, w_gate, w_up, w_down):
    b, n, _ = h.shape
    cap = CAPACITY_FACTOR * n // N_EXPERTS
    aff = jax.nn.softmax((h @ w_r).astype(jnp.float32), axis=-1)
    gates, idx = lax.top_k(jnp.swapaxes(aff, 1, 2), cap)
    bidx = jnp.arange(b)[:, None, None]
    xg = h[bidx, idx]
    hid = jax.nn.silu(jnp.einsum('becd,edf->becf', xg, w_gate)) * jnp.einsum('becd,edf->becf', xg, w_up)
    ye = jnp.einsum('becf,efd->becd', hid, w_down) * gates[..., None].astype(h.dtype)
    return jnp.zeros_like(h).at[bidx, idx].add(ye)


def setup_inputs(seed: int = 0) -> dict:
    key = jax.random.key(seed)
    ks = jax.random.split(key, 24)
    nrm = jax.random.normal
    f32 = jnp.float32
    d = D_MODEL
    return {
        'x': nrm(ks[0], (BATCH, SEQ, d), f32),
        'c': nrm(ks[1], (BATCH, d), f32),
        'ctx': nrm(ks[2], (BATCH, CTX_LEN, d), f32),
        'c_ctx': nrm(ks[3], (d,), f32),
        'ada_w': nrm(ks[4], (DEPTH, d, 6 * d), f32) * (0.5 * d ** -0.5),
        'ada_b': nrm(ks[5], (DEPTH, 6 * d), f32) * 0.02,
        'ln_g': 1.0 + 0.02 * nrm(ks[6], (DEPTH, 2, d), f32),
        'ln_b': 0.02 * nrm(ks[7], (DEPTH, 2, d), f32),
        'attn_in_w': nrm(ks[8], (N_EVEN, d, IN_DIM), f32) * d ** -0.5,
        'attn_in_b': 0.02 * nrm(ks[9], (N_EVEN, IN_DIM), f32),
        'q_norm_g': 1.0 + 0.02 * nrm(ks[10], (N_EVEN, HEAD_DIM), f32),
        'k_norm_g': 1.0 + 0.02 * nrm(ks[11], (N_EVEN, HEAD_DIM), f32),
        'conv_w': nrm(ks[12], (N_EVEN, CONV_WIDTH, CONV_DIM), f32) * CONV_WIDTH ** -0.5,
        'conv_b': 0.02 * nrm(ks[13], (N_EVEN, CONV_DIM), f32),
        'conv_ln_g': 1.0 + 0.02 * nrm(ks[14], (N_EVEN, CONV_DIM), f32),
        'conv_ln_b': 0.02 * nrm(ks[15], (N_EVEN, CONV_DIM), f32),
        'attn_out_w': nrm(ks[16], (N_EVEN, MIX_DIM, d), f32) * (MIX_DIM ** -0.5 * DEEPNORM_BETA),
        'attn_out_b': 0.02 * nrm(ks[17], (N_EVEN, d), f32),
        'fourier_out_w': nrm(ks[18], (N_ODD, d, d), f32) * (d ** -0.5 * DEEPNORM_BETA),
        'fourier_out_b': 0.02 * nrm(ks[19], (N_ODD, d), f32),
        'router_w': nrm(ks[20], (DEPTH, d, N_EXPERTS), f32) * d ** -0.5,
        'expert_w_gate': nrm(ks[21], (DEPTH, N_EXPERTS, d, EXPERT_FF), f32) * d ** -0.5,
        'expert_w_up': nrm(ks[22], (DEPTH, N_EXPERTS, d, EXPERT_FF), f32) * d ** -0.5,
        'expert_w_down': nrm(ks[23], (DEPTH, N_EXPERTS, EXPERT_FF, d), f32) * (EXPERT_FF ** -0.5 * DEEPNORM_BETA),
    }


def reference(x, c, ctx, c_ctx, ada_w, ada_b, ln_g, ln_b, attn_in_w, attn_in_b, q_norm_g, k_norm_g,
              conv_w, conv_b, conv_ln_g, conv_ln_b, attn_out_w, attn_out_b, fourier_out_w, fourier_out_b,
              router_w, expert_w_gate, expert_w_up, expert_w_down):
    n_tok = x.shape[1]
    cos, sin = axial_rope_tables(n_tok)
    mod_lat = jnp.einsum('bd,lde->lbe', jax.nn.silu(c), ada_w) + ada_b[:, None, :]
    mod_ctx = jnp.einsum('d,lde->le', jax.nn.silu(c_ctx), ada_w) + ada_b
    xl, xc = x, ctx
    for i in range(DEPTH):
        need_ctx = i < DEPTH - 1
        ml = jnp.split(mod_lat[i][:, None, :], 6, axis=-1)
        mc = jnp.split(mod_ctx[i], 6, axis=-1)
        hl = xl * (1 + ml[1]) + ml[0]
        hc = xc * (1 + mc[1]) + mc[0]
        j = i // 2
        if i % 2 == 0:
            yl, yc = mixer_attn_conv(hl, hc, cos, sin, attn_in_w[j], attn_in_b[j], q_norm_g[j], k_norm_g[j],
                                     conv_w[j], conv_b[j], conv_ln_g[j], conv_ln_b[j],
                                     attn_out_w[j], attn_out_b[j], need_ctx)
        else:
            yl = fourier_mix(hl) @ fourier_out_w[j] + fourier_out_b[j]
            yc = (fourier_mix(hc) @ fourier_out_w[j] + fourier_out_b[j]) if need_ctx else None
        xl = layer_norm(DEEPNORM_ALPHA * xl + ml[2] * yl, ln_g[i, 0], ln_b[i, 0])
        hl = xl * (1 + ml[4]) + ml[3]
        yl = expert_choice_moe(hl, router_w[i], expert_w_gate[i], expert_w_up[i], expert_w_down[i])
        xl = layer_norm(DEEPNORM_ALPHA * xl + ml[5] * yl, ln_g[i, 1], ln_b[i, 1])
        if need_ctx:
            xc = layer_norm(DEEPNORM_ALPHA * xc + mc[2] * yc, ln_g[i, 0], ln_b[i, 0])
            hc = xc * (1 + mc[4]) + mc[3]
            yc = expert_choice_moe(hc, router_w[i], expert_w_gate[i], expert_w_up[i], expert_w_down[i])
            xc = layer_norm(DEEPNORM_ALPHA * xc + mc[5] * yc, ln_g[i, 1], ln_b[i, 1])
    return xl
```

```python
import numpy as np
from contextlib import ExitStack
import concourse.bass as bass
import concourse.mybir as mybir
from concourse.bass_utils import run_bass_kernel_spmd

F32 = mybir.dt.float32
BF16 = mybir.dt.bfloat16
I32 = mybir.dt.int32
AF = mybir.ActivationFunctionType
ALU = mybir.AluOpType
AX = mybir.AxisListType

D = 1024
KC = 8
CTX = 256
NE = 16
NH = 8
NKV = 2
HD = 64
CONVD = 512
CW = 31
NG = 4
EPS = 1e-6
BIG = 1.0e6


_UC = [0]


def _u(name):
    _UC[0] += 1
    return 'sb_%s_%d' % (name, _UC[0])


class Res:
    __slots__ = ("name", "w", "r")

    def __init__(self, name=""):
        self.name = name
        self.w = None
        self.r = []


class FW:
    def __init__(self, nc, n_dma_sems=32):
        self.nc = nc
        self.engs = {"pe": nc.tensor, "act": nc.scalar, "dve": nc.vector, "pool": nc.gpsimd, "sp": nc.sync}
        self.sems, self.cnt, self._ctx = {}, {}, []
        for k in self.engs:
            cm = nc.semaphore("s_" + k)
            self.sems[k] = cm.__enter__()
            self._ctx.append(cm)
            self.cnt[k] = 0
        self.dma_sems = []
        for i in range(n_dma_sems):
            cm = nc.semaphore("d_%d" % i)
            self.sems["dma%d" % i] = cm.__enter__()
            self._ctx.append(cm)
            self.cnt["dma%d" % i] = 0
            self.dma_sems.append("dma%d" % i)
        self.dma_rr = 0
        self.waited = {k: {} for k in self.engs}
        self.n_inst = 0

    def close(self):
        for cm in reversed(self._ctx):
            cm.__exit__(None, None, None)

    def _wait(self, eng, dep):
        if dep is None:
            return
        key, val = dep
        w = self.waited[eng]
        if w.get(key, 0) >= val:
            return
        if key == eng and eng == "pe":
            return
        self.engs[eng].wait_ge(self.sems[key], val)
        w[key] = val

    def deps(self, eng, reads, writes):
        for r in reads:
            self._wait(eng, r.w)
        for wr in writes:
            self._wait(eng, wr.w)
            for d in wr.r:
                self._wait(eng, d)

    def _done(self, tok, reads, writes):
        for r in reads:
            r.r.append(tok)
            if len(r.r) > 24:
                best = {}
                for kk, vv in r.r:
                    if best.get(kk, 0) < vv:
                        best[kk] = vv
                r.r = list(best.items())
        for wr in writes:
            wr.w = tok
            wr.r = []
        self.n_inst += 1

    def op(self, eng, fn, reads=(), writes=()):
        self.deps(eng, reads, writes)
        inst = fn()
        self.cnt[eng] += 1
        inst.then_inc(self.sems[eng], 1)
        tok = (eng, self.cnt[eng])
        self._done(tok, reads, writes)
        return tok

    def op_nt(self, eng, fn, reads=(), writes=()):
        self.deps(eng, reads, writes)
        fn()
        self.n_inst += 1

    def dma(self, eng, fn, reads=(), writes=()):
        key = self.dma_sems[self.dma_rr]
        self.dma_rr = (self.dma_rr + 1) % len(self.dma_sems)
        if self.cnt[key] > 0:
            self._wait(eng, (key, self.cnt[key]))
        self.deps(eng, reads, writes)
        inst = fn()
        self.cnt[key] += 16
        inst.then_inc(self.sems[key], 16)
        tok = (key, self.cnt[key])
        self._done(tok, reads, writes)
        return tok

    def war(self, eng, res):
        self._wait(eng, res.w)
        for d in res.r:
            self._wait(eng, d)
        res.r = []

    def barrier(self):
        for eng in self.engs:
            for key, c in self.cnt.items():
                if c > 0:
                    self._wait(eng, (key, c))

    def drain(self, eng="sp"):
        for key, c in self.cnt.items():
            if c > 0:
                self._wait(eng, (key, c))


class Phase(ExitStack):
    def __init__(self, fw):
        super().__init__()
        self._fw = fw

    def __exit__(self, *a):
        self._fw.barrier()
        return super().__exit__(*a)


class Ring:
    def __init__(self, nc, es, name, shape, dt, n, psum=False):
        self.items = []
        for i in range(n):
            alloc = nc.psum_tensor if psum else nc.sbuf_tensor
            t = es.enter_context(alloc(_u("%s_%d" % (name, i)), shape, dt))
            self.items.append((t, Res("%s_%d" % (name, i))))
        self.i = 0

    def next(self):
        it = self.items[self.i]
        self.i = (self.i + 1) % len(self.items)
        return it


def _bf16(a):
    import ml_dtypes
    return np.ascontiguousarray(a).astype(ml_dtypes.bfloat16)


class K:
    def __init__(self, SEQ, FF, DEPTH):
        self.SEQ, self.FF, self.DEPTH = SEQ, FF, DEPTH
        self.T = SEQ + CTX
        self.NT = self.T // 128
        self.FC = FF // 128
        self.CAP = 2 * SEQ // NE
        self.CAPC = 2 * CTX // NE
        self.NS = self.CAP + self.CAPC
        self.B = SEQ // 128
        self.blocks = [(i * 256, 256, False) for i in range(SEQ // 256)] + [(SEQ, CTX, True)]
        self.alpha = (2 * DEPTH) ** 0.25

    def host_consts(self):
        SEQ, B = self.SEQ, self.B
        c = {}
        c["ident_f"] = np.eye(128, dtype=np.float32)
        c["ident_b"] = _bf16(np.eye(128))
        c["ones_d"] = np.full((128, 128), 1.0 / D, np.float32)
        c["ones_c"] = np.full((128, 128), 1.0 / CONVD, np.float32)
        c["ones16"] = np.ones((16, 16), np.float32)
        rows = SEQ // 64
        row = np.repeat(np.arange(rows, dtype=np.float32), 64)
        col = np.tile(np.arange(64, dtype=np.float32), rows)
        inv = (10000.0 ** (-np.arange(0, 32, 2, dtype=np.float32) / 32)).astype(np.float32)
        ang = np.concatenate([row[:, None] * inv, col[:, None] * inv], axis=-1).astype(np.float32)
        c["rope_c"] = np.cos(ang).astype(np.float32)
        c["rope_s"] = np.sin(ang).astype(np.float32)
        n = np.arange(256, dtype=np.float64)
        th = 2 * np.pi * np.outer(n, n) / 256.0
        wc = np.concatenate([np.cos(th), -np.sin(th)], axis=1) / 16.0
        c["wc"] = _bf16(wc.reshape(2, 128, 512).transpose(1, 0, 2))
        a = np.arange(128, dtype=np.float64)
        th = 2 * np.pi * np.outer(a, a) / 128.0
        c["t1"] = _bf16(np.concatenate([np.cos(th), -np.sin(th)], axis=1))
        c["t2"] = _bf16(np.concatenate([np.sin(th), np.cos(th)], axis=1))
        b = np.arange(B, dtype=np.float64)[:, None, None]
        cc = np.arange(128, dtype=np.float64)[None, :, None]
        d = np.arange(B, dtype=np.float64)[None, None, :]
        th = 2 * np.pi * b * (cc + 128 * d) / SEQ
        sc = 1.0 / np.sqrt(SEQ)
        c["mcos"] = _bf16(np.cos(th) * sc)
        c["msin"] = _bf16(np.sin(th) * sc)
        c["c256"] = _bf16((np.cos(2 * np.pi * np.outer(n, n) / 256) / 16.0).reshape(2, 128, 256).transpose(1, 0, 2))
        c["s256"] = _bf16((np.sin(2 * np.pi * np.outer(n, n) / 256) / 16.0).reshape(2, 128, 256).transpose(1, 0, 2))
        c["tokid"] = (np.arange(128)[:, None] + 128 * np.arange(self.NT)[None, :]).astype(np.int32)
        return c

    def build(self):
        SEQ, T, FF, DEPTH, NT, FC = self.SEQ, self.T, self.FF, self.DEPTH, self.NT, self.FC
        NEV, NOD = (DEPTH + 1) // 2, DEPTH // 2
        nc = bass.Bass("TRN2", target_bir_lowering=False)
        self.nc = nc
        fw = FW(nc)
        self.fw = fw
        es = ExitStack()

        def din(name, shape, dt=F32):
            return nc.dram_tensor(name, list(shape), dt, kind="ExternalInput").ap()

        def dscr(name, shape, dt=F32):
            if getattr(self, "debug", False):
                return nc.dram_tensor(name, list(shape), dt, kind="ExternalOutput").ap()
            return nc.dram_tensor(name, list(shape), dt).ap()

        I = {}
        hc = self.host_consts()
        for nm, arr in hc.items():
            dt = {np.dtype(np.float32): F32, np.dtype(np.int32): I32}.get(arr.dtype, BF16)
            I[nm] = din(nm, arr.shape, dt)
        I["xT"] = din("xT", [128, KC, T])
        I["cT"] = din("cT", [128, KC, 4])
        I["ada_w"] = din("ada_w", [DEPTH, D, 6 * D])
        I["ada_b"] = din("ada_b", [128, DEPTH, 48])
        I["ln_g"] = din("ln_g", [128, DEPTH, 2, KC])
        I["ln_b"] = din("ln_b", [128, DEPTH, 2, KC])
        I["in_w"] = din("in_w", [NEV, D, 1792])
        I["in_b_tm"] = din("in_b_tm", [128, NEV, 768])
        I["in_b_fm"] = din("in_b_fm", [128, NEV, 8])
        I["qk_g"] = din("qk_g", [128, NEV, 10 * HD])
        I["conv_w"] = din("conv_w", [128, NEV, 4, CW])
        I["conv_p"] = din("conv_p", [128, NEV, 3, 4])
        I["out_w"] = din("out_w", [NEV, D, D])
        I["out_b"] = din("out_b", [128, NEV, KC])
        if NOD:
            I["fo_w"] = din("fo_w", [NOD, D, D])
            I["fo_b"] = din("fo_b", [128, NOD, KC])
        I["r_w"] = din("r_w", [DEPTH, D, NE])
        I["w_gate"] = din("w_gate", [DEPTH, NE, D, FF])
        I["w_up"] = din("w_up", [DEPTH, NE, D, FF])
        I["w_down"] = din("w_down", [DEPTH, NE, FF, D])
        self.I = I
        OUT = nc.dram_tensor("yT", [128, KC, SEQ], F32, kind="ExternalOutput").ap()

        XR = dscr("XR", [128, KC, T]); rXR = Res("XR")
        QT = dscr("QT", [HD, NH, T], BF16); rQT = Res("QT")
        UT = dscr("UT", [128, 4, T + 64], BF16); rUT = Res("UT")
        Z = dscr("Z", [T, 2 * D], BF16); rZ = Res("Z")
        FN = dscr("FN", [T, D], BF16); rFN = Res("FN")
        RW = 1024 + 2 + 2 * NE
        self.RW = RW
        HM = dscr("HM", [T, RW], BF16); rHM = Res("HM")
        NSP = ((self.NS + 127) // 128) * 128
        XG = [dscr("XG%d" % e, [NSP, RW], BF16) for e in range(NE)]; rXG = [Res("XG%d" % e) for e in range(NE)]
        Y = dscr("Y", [T, D]); rY = Res("Y")

        sb = lambda name, shape, dt=F32: es.enter_context(nc.sbuf_tensor(_u(name), list(shape), dt))
        ident_f, ident_b = sb("ident_f", [128, 128]), sb("ident_b", [128, 128], BF16)
        ones_d, ones_c, ones16 = sb("ones_d", [128, 128]), sb("ones_c", [128, 128]), sb("ones16", [16, 16])
        tokid = sb("tokid", [128, NT], I32)
        modv = sb("modv", [128, DEPTH, 6, KC, 2])
        lng, lnb = sb("lng", [128, DEPTH, 2, KC]), sb("lnb", [128, DEPTH, 2, KC])
        zeros = sb("zeros", [128, 1024])
        rC = Res("consts")
        qi = [0]
        hwq = [("sp", nc.sync), ("act", nc.scalar)]

        def ld(out, in_, reads=(), writes=()):
            nm, q = hwq[qi[0] % 2]
            qi[0] += 1
            return fw.dma(nm, lambda: q.dma_start(out=out, in_=in_), reads=reads, writes=writes)

        def ldc(out, in_, reads=(), writes=()):
            return fw.dma("pool", lambda: nc.gpsimd.dma_start(out=out, in_=in_), reads=reads, writes=writes)

        for t, nm in ((ident_f, "ident_f"), (ident_b, "ident_b"), (ones_d, "ones_d"), (ones_c, "ones_c"),
                      (ones16, "ones16"), (tokid, "tokid"), (lng, "ln_g"), (lnb, "ln_b")):
            ld(t[:], I[nm], writes=[rC])
        fw.op("pool", lambda: nc.gpsimd.memset(zeros[:], 0.0), writes=[rC])

        PS = Ring(nc, es, "ps", [128, 512], F32, 6, psum=True)
        PO = Ring(nc, es, "po", [128, 512], F32, 2, psum=True)

        V = nc.vector
        A = nc.scalar
        PE = nc.tensor
        G = nc.gpsimd

        def mm_group(out, pairs, reads, writes, first=True, last=True):
            n = len(pairs)
            for i, (l, r) in enumerate(pairs):
                st, sp_ = (first and i == 0), (last and i == n - 1)
                f = lambda l=l, r=r, st=st, sp_=sp_: PE.matmul(out, lhsT=l, rhs=r, start=st, stop=sp_,
                                                              skip_group_check=True)
                if i == n - 1:
                    fw.op("pe", f, reads=reads, writes=writes)
                else:
                    fw.op_nt("pe", f, reads=reads, writes=writes)

        with Phase(fw) as ps_:
            cs = ps_.enter_context(nc.sbuf_tensor(_u("cs"), [128, KC, 4], F32)); rcs = Res("cs")
            adb = ps_.enter_context(nc.sbuf_tensor(_u("adb"), [128, DEPTH, 48], F32)); radb = Res("adb")
            wr = Ring(nc, ps_, "adw", [128, KC, 512], F32, 2)
            rmod = Res("modv")
            ld(cs[:], I["cT"], writes=[rcs])
            ld(adb[:], I["ada_b"], writes=[radb])
            fw.op("act", lambda: A.activation(out=cs[:], in_=cs[:], func=AF.Silu), reads=[rcs], writes=[rcs])
            for l in range(DEPTH):
                for cb in range(12):
                    wt, rw = wr.next()
                    ld(wt[:], I["ada_w"][l, :, cb * 512:(cb + 1) * 512].rearrange("(c p) n -> p c n", p=128), writes=[rw])
                    pt, rp = PS.next()
                    for j in range(4):
                        mm_group(pt[:, 4 * j:4 * j + 4],
                                 [(wt[:, kc, j * 128:(j + 1) * 128], cs[:, kc, :]) for kc in range(KC)],
                                 reads=[rw, rcs], writes=[rp])
                    for j in range(4):
                        jj = cb * 4 + j
                        six, ch = jj // 8, jj % 8
                        fw.op("dve", lambda j=j, jj=jj, six=six, ch=ch: V.tensor_scalar(
                            out=modv[:, l, six, ch, :], in0=pt[:, 4 * j:4 * j + 2], scalar1=adb[:, l, jj:jj + 1],
                            scalar2=(1.0 if six in (1, 4) else 0.0), op0=ALU.add, op1=ALU.add),
                            reads=[rp, radb], writes=[rmod])
            self.rmod = rmod
            if getattr(self, "debug", False):
                DBGM = dscr("DBGM", [128, DEPTH * 6 * KC * 2])
                ld(DBGM, modv[:].rearrange("p l s c t -> p (l s c t)"), reads=[rmod])

        def mv(l, six, ch, col):
            return modv[:, l, six, ch, col:col + 1]

        dv = sb("dv", [128, 8, KC, 2]); rdv = Res("dv")

        TR = {}
        tcnt = [0]

        def alloc_tail(es_, lite=False, depth=1):
            tcnt[0] += 1
            sfx = "_%d" % tcnt[0]
            TR["xinR"] = Ring(nc, es_, "xin" + sfx, [128, KC, 256], F32, 2)
            TR["hTR"] = Ring(nc, es_, "hT" + sfx, [128, KC, 256], BF16, depth)
            if lite:
                return
            TR["zR"] = Ring(nc, es_, "z" + sfx, [128, KC, 256], F32, 2)
            TR["sqR"] = Ring(nc, es_, "sq" + sfx, [128, KC, 256], F32, depth)
            TR["stR"] = Ring(nc, es_, "st" + sfx, [128, 2, 256], F32, depth)
            TR["xoR"] = Ring(nc, es_, "xo" + sfx, [128, KC, 256], F32, depth)
            TR["hmR"] = Ring(nc, es_, "hmT" + sfx, [128, KC, 256], BF16, depth)
            TR["hrR"] = Ring(nc, es_, "hrow" + sfx, [128, 2, 1024], BF16, depth)

        def ln_fm(zt, rz, nch, nt, ones_t):
            sq, rsq = TR['sqR'].next()
            for c_ in range(nch):
                fw.op("act", lambda c_=c_: A.activation(out=sq[:, c_, :nt], in_=zt[:, c_, :nt], func=AF.Square),
                      reads=[rz], writes=[rsq])
            pm, rpm = PS.next()
            pq, rpq = PS.next()
            mm_group(pm[:, :nt], [(ones_t[:], zt[:, c_, :nt]) for c_ in range(nch)], reads=[rz, rC], writes=[rpm])
            mm_group(pq[:, :nt], [(ones_t[:], sq[:, c_, :nt]) for c_ in range(nch)], reads=[rsq, rC], writes=[rpq])
            st, rst = TR['stR'].next()
            fw.op("dve", lambda: V.tensor_copy(out=st[:, 0, :nt], in_=pm[:, :nt]), reads=[rpm], writes=[rst])
            fw.op("dve", lambda: V.tensor_tensor(out=st[:, 1, :nt], in0=st[:, 0, :nt], in1=st[:, 0, :nt], op=ALU.mult),
                  reads=[rst], writes=[rst])
            fw.op("dve", lambda: V.tensor_tensor(out=st[:, 1, :nt], in0=pq[:, :nt], in1=st[:, 1, :nt], op=ALU.subtract),
                  reads=[rpq, rst], writes=[rst])
            fw.op("dve", lambda: V.tensor_scalar(out=st[:, 1, :nt], in0=st[:, 1, :nt], scalar1=0.0, scalar2=EPS,
                                                 op0=ALU.max, op1=ALU.add), reads=[rst], writes=[rst])
            fw.op("act", lambda: A.activation(out=st[:, 1, :nt], in_=st[:, 1, :nt], func=AF.Sqrt), reads=[rst], writes=[rst])
            fw.op("dve", lambda: V.reciprocal(out=st[:, 1, :nt], in_=st[:, 1, :nt]), reads=[rst], writes=[rst])
            for c_ in range(nch):
                fw.op("dve", lambda c_=c_: V.tensor_tensor(out=zt[:, c_, :nt], in0=zt[:, c_, :nt], in1=st[:, 0, :nt],
                                                         op=ALU.subtract), reads=[rz, rst], writes=[rz])
                fw.op("pool", lambda c_=c_: G.tensor_tensor(out=zt[:, c_, :nt], in0=zt[:, c_, :nt], in1=st[:, 1, :nt],
                                                          op=ALU.mult), reads=[rz, rst], writes=[rz])

        LG = dscr("LG", [NE, T]); rlog = Res("LG")
        lgR = Ring(nc, es, "lgt", [NE, 256], F32, 2)
        rw_b = sb("rw_b", [128, KC, NE], BF16); rrw = Res("rw")

        def mixer_tail(l, yps, t0, nt, isctx, xin, rxin, src=None):
            col = 1 if isctx else 0
            zt, rz = TR['zR'].next()
            for c_ in range(KC):
                pa, rp = yps(c_)
                fw.op("act", lambda c_=c_, pa=pa: A.activation(out=zt[:, c_, :nt], in_=pa, func=AF.Identity,
                                                              bias=dv[:, 0, c_, col:col + 1], scale=mv(l, 2, c_, col)),
                      reads=[rp, rdv, self.rmod], writes=[rz])
                fw.op("dve", lambda c_=c_: V.scalar_tensor_tensor(out=zt[:, c_, :nt], in0=xin[:, c_, :nt],
                                                                 scalar=self.alpha, in1=zt[:, c_, :nt],
                                                                 op0=ALU.mult, op1=ALU.add),
                      reads=[rxin, rz], writes=[rz])
            ln_fm(zt, rz, KC, nt, ones_d)
            xo, rxo = TR['xoR'].next()
            hm, rhm = TR['hmR'].next()
            for c_ in range(KC):
                fw.op("act", lambda c_=c_: A.activation(out=xo[:, c_, :nt], in_=zt[:, c_, :nt], func=AF.Identity,
                                                       bias=lnb[:, l, 0, c_:c_ + 1], scale=lng[:, l, 0, c_:c_ + 1]),
                      reads=[rz, rC], writes=[rxo])
                fw.op("act", lambda c_=c_: A.activation(out=hm[:, c_, :nt], in_=zt[:, c_, :nt], func=AF.Identity,
                                                       bias=dv[:, 2, c_, col:col + 1], scale=dv[:, 1, c_, col:col + 1]),
                      reads=[rz, rdv], writes=[rhm])
            ld(XR[:, :, t0:t0 + nt], xo[:, :, :nt], reads=[rxo], writes=[rXR])
            pl, rpl = PS.next()
            mm_group(pl[:NE, :nt], [(rw_b[:, kc, :], hm[:, kc, :nt]) for kc in range(KC)], reads=[rrw, rhm], writes=[rpl])
            lgt, rlgt = lgR.next()
            fw.op("dve", lambda: V.tensor_copy(out=lgt[:, :nt], in_=pl[:NE, :nt]), reads=[rpl], writes=[rlgt])
            ld(LG[:, t0:t0 + nt], lgt[:, :nt], reads=[rlgt], writes=[rlog])
            hr, rhr = TR['hrR'].next()
            for s in range(nt // 128):
                for half in range(2):
                    pt, rp = PS.next()
                    ptb = pt[:].bitcast(BF16)
                    for j in range(4):
                        c_ = half * 4 + j
                        fw.op("pe", lambda c_=c_, j=j: PE.transpose(ptb[:, j * 128:(j + 1) * 128],
                                                                    hm[:, c_, s * 128:(s + 1) * 128], ident_b[:]),
                              reads=[rhm, rC], writes=[rp])
                    fw.op("dve" if half == 0 else "pool" if False else "dve",
                          lambda half=half, s=s: V.tensor_copy(out=hr[:, s, half * 512:(half + 1) * 512], in_=ptb[:, 0:512]),
                          reads=[rp], writes=[rhr])
            ld(HM[t0:t0 + nt, 0:1024].rearrange("(s p) n -> p s n", p=128), hr[:, :nt // 128, :], reads=[rhr], writes=[rHM])

        def layer_vectors(l, b_out_ap):
            for col in range(2):
                fw.op("dve", lambda col=col: V.tensor_tensor(out=dv[:, 0, :, col], in0=modv[:, l, 2, :, col],
                                                            in1=b_out_ap, op=ALU.mult), reads=[self.rmod, rC], writes=[rdv])
                fw.op("dve", lambda col=col: V.tensor_tensor(out=dv[:, 1, :, col], in0=modv[:, l, 4, :, col],
                                                            in1=lng[:, l, 0, :], op=ALU.mult), reads=[self.rmod, rC], writes=[rdv])
                fw.op("dve", lambda col=col: V.tensor_tensor(out=dv[:, 2, :, col], in0=modv[:, l, 4, :, col],
                                                            in1=lnb[:, l, 0, :], op=ALU.mult), reads=[self.rmod, rC], writes=[rdv])
                fw.op("dve", lambda col=col: V.tensor_tensor(out=dv[:, 2, :, col], in0=dv[:, 2, :, col],
                                                            in1=modv[:, l, 3, :, col], op=ALU.add), reads=[self.rmod, rdv], writes=[rdv])

        self.breg = nc.gpsimd.to_reg(self.NS - 1)
        XSRC = [I["xT"]]

        def load_x(t0, nt):
            xin, rxin = TR['xinR'].next()
            ld(xin[:, :, :nt], XSRC[0][:, :, t0:t0 + nt], reads=[rXR], writes=[rxin])
            return xin, rxin


        def mod1(l, xin, rxin, nt, isctx):
            col = 1 if isctx else 0
            hT, rh = TR['hTR'].next()
            for c_ in range(KC):
                fw.op("act", lambda c_=c_: A.activation(out=hT[:, c_, :nt], in_=xin[:, c_, :nt], func=AF.Identity,
                                                       bias=mv(l, 0, c_, col), scale=mv(l, 1, c_, col)),
                      reads=[rxin, self.rmod], writes=[rh])
            return hT, rh

        def load_x2(t0, nt):
            xin, rxin = TR['xinR'].next()
            ld(xin[:, :, :nt], XR[:, :, t0:t0 + nt], reads=[rXR], writes=[rxin])
            return xin, rxin

        rmod_ = self.rmod
        wout = sb("wout", [128, KC, D], BF16); rwout = Res("wout")
        bo = sb("bo", [128, KC]);

        for l in range(DEPTH):
            j = l // 2
            ldc(rw_b[:], I["r_w"][l].rearrange("(c p) n -> p c n", p=128), writes=[rrw])
            if l % 2 == 0:
                self.even_layer(l, j, locals())
            else:
                self.odd_layer(l, j, locals())
            self.moe_layer(l, locals())
            XSRC[0] = XR

        with Phase(fw) as fin:
            alloc_tail(fin)
            for (t0, nt, isctx) in self.blocks:
                if isctx:
                    continue
                xin, rxin = load_x(t0, nt)
                ld(OUT[:, :, t0:t0 + nt], xin[:, :, :nt], reads=[rxin])
        fw.drain("sp")
        es.close()
        fw.close()
        return nc


class _NS:
    def __init__(self, d):
        self.__dict__.update(d)


def bc(ap, shape):
    return ap.unsqueeze(2).to_broadcast(list(shape))


def even_layer(self, l, j, Ld):
    L = _NS(Ld)
    nc, fw, I, PS, V, A, PE, G = L.nc, L.fw, L.I, L.PS, L.V, L.A, L.PE, L.G
    mm_group, ld, ldc, rC = L.mm_group, L.ld, L.ldc, L.rC
    SEQ, T, NT = self.SEQ, self.T, self.NT
    with Phase(fw) as es:
        sb = lambda name, shape, dt=F32: es.enter_context(nc.sbuf_tensor(_u(name), list(shape), dt))
        KT = sb("KT", [128, NKV, T], BF16); rKT = Res("KT")
        fw.op("pool", lambda: G.memset(KT[HD:128, :, :], 0.0), writes=[rKT])
        Vt = sb("Vt", [128, NT, NKV, HD + 1], BF16); rVt = Res("Vt")
        rP = Res("eparams")
        ldc(L.wout[:], I["out_w"][j].rearrange("(c p) n -> p c n", p=128), writes=[L.rwout])
        ld(L.bo[:], I["out_b"][:, j, :], reads=[L.rdv], writes=[rC])
        L.layer_vectors(l, L.bo[:])
        fw.op("pool", lambda: G.memset(Vt[:, :, :, HD:HD + 1], 1.0), writes=[rVt])
        if l == 0:
            zb = L.zeros[:].bitcast(BF16)
            for (a0, n0) in ((0, 16), (16 + SEQ, 32), (SEQ + 48 + CTX, 16)):
                ld(L.UT[:, :, a0:a0 + n0], zb[:, 0:4 * n0].rearrange("p (c n) -> p c n", c=4), reads=[rC], writes=[L.rUT])
        ucol = lambda t0, isctx: (SEQ + 48 + (t0 - SEQ)) if isctx else (16 + t0)

        with Phase(fw) as e1:
            L.alloc_tail(e1, lite=True)
            sb1 = lambda name, shape, dt=F32: e1.enter_context(nc.sbuf_tensor(_u(name), list(shape), dt))
            w_in = sb1("w_in", [128, KC, 1792], BF16); rwin = Res("w_in")
            btm = sb1("btm", [128, 768]); bfm = sb1("bfm", [128, 8]); qkg = sb1("qkg", [128, 10, HD])
            ldc(w_in[:], I["in_w"][j].rearrange("(c p) n -> p c n", p=128), writes=[rwin])
            ld(btm[:], I["in_b_tm"][:, j, :], writes=[rP]); ld(bfm[:], I["in_b_fm"][:, j, :], writes=[rP])
            ld(qkg[:], I["qk_g"][:, j, :].rearrange("p (h d) -> p h d", d=HD), writes=[rP])
            ropR = Ring(nc, e1, "rop", [128, 2, 32], F32, 2)
            agR = Ring(nc, e1, "ag", [128, 256], F32, 2)
            sgR = Ring(nc, e1, "sgm", [128, 256], F32, 2)
            uR = Ring(nc, e1, "uT", [128, 4, 256], BF16, 2)
            qkR = Ring(nc, e1, "qk", [128, 10, HD], F32, 2)
            sqR2 = Ring(nc, e1, "qsq", [128, 10, HD], F32, 2)
            ssR = Ring(nc, e1, "ss", [128, 10], F32, 2)
            rtR = Ring(nc, e1, "rt", [128, 10, 32], F32, 4)
            qbR = Ring(nc, e1, "qb", [128, 10, HD], BF16, 2)
            qtR = Ring(nc, e1, "qTs", [HD, NH, 256], BF16, 2)
            for (t0, nt, isctx) in self.blocks:
                xin, rxin = L.load_x(t0, nt)
                hT, rh = L.mod1(l, xin, rxin, nt, isctx)
                uT, ru = uR.next()
                for cc in range(4):
                    pa, rpa = PS.next()
                    mm_group(pa[:, :nt], [(w_in[:, kc, 768 + cc * 128:768 + (cc + 1) * 128], hT[:, kc, :nt]) for kc in range(KC)],
                             reads=[rwin, rh], writes=[rpa])
                    pg, rpg = PS.next()
                    mm_group(pg[:, :nt], [(w_in[:, kc, 1280 + cc * 128:1280 + (cc + 1) * 128], hT[:, kc, :nt]) for kc in range(KC)],
                             reads=[rwin, rh], writes=[rpg])
                    at, rat = agR.next()
                    sg, rsg = sgR.next()
                    fw.op("act", lambda: A.activation(out=at[:, :nt], in_=pa[:, :nt], func=AF.Identity, bias=bfm[:, cc:cc + 1]),
                          reads=[rpa, rP], writes=[rat])
                    fw.op("act", lambda: A.activation(out=sg[:, :nt], in_=pg[:, :nt], func=AF.Sigmoid, bias=bfm[:, 4 + cc:5 + cc]),
                          reads=[rpg, rP], writes=[rsg])
                    fw.op("dve", lambda: V.tensor_tensor(out=uT[:, cc, :nt], in0=at[:, :nt], in1=sg[:, :nt], op=ALU.mult),
                          reads=[rat, rsg], writes=[ru])
                uc = ucol(t0, isctx)
                ld(L.UT[:, :, uc:uc + nt], uT[:, :, :nt], reads=[ru], writes=[L.rUT])
                qTs, rqT = qtR.next()
                for s in range(nt // 128):
                    ti = (t0 + s * 128) // 128
                    pq, rpq = PS.next()
                    mm_group(pq[:, :], [(hT[:, kc, s * 128:(s + 1) * 128], w_in[:, kc, 0:512]) for kc in range(KC)],
                             reads=[rwin, rh], writes=[rpq])
                    pk, rpk = PS.next()
                    mm_group(pk[:, 0:256], [(hT[:, kc, s * 128:(s + 1) * 128], w_in[:, kc, 512:768]) for kc in range(KC)],
                             reads=[rwin, rh], writes=[rpk])
                    qk, rqk = qkR.next()
                    fw.op("dve", lambda: V.tensor_tensor(out=qk[:, 0:8, :], in0=pq[:, :].rearrange("p (h d) -> p h d", d=HD),
                                                         in1=btm[:, 0:512].rearrange("p (h d) -> p h d", d=HD), op=ALU.add),
                          reads=[rpq, rP], writes=[rqk])
                    fw.op("dve", lambda: V.tensor_tensor(out=qk[:, 8:10, :], in0=pk[:, 0:128].rearrange("p (h d) -> p h d", d=HD),
                                                         in1=btm[:, 512:640].rearrange("p (h d) -> p h d", d=HD), op=ALU.add),
                          reads=[rpk, rP], writes=[rqk])
                    fw.op("dve", lambda: V.tensor_tensor(out=Vt[:, ti, :, 0:HD], in0=pk[:, 128:256].rearrange("p (h d) -> p h d", d=HD),
                                                         in1=btm[:, 640:768].rearrange("p (h d) -> p h d", d=HD), op=ALU.add),
                          reads=[rpk, rP], writes=[rVt])
                    sq, rsq = sqR2.next()
                    ss, rss = ssR.next()
                    fw.op("act", lambda: A.activation(out=sq[:], in_=qk[:], func=AF.Square), reads=[rqk], writes=[rsq])
                    fw.op("dve", lambda: V.reduce_sum(out=ss[:], in_=sq[:], axis=AX.X), reads=[rsq], writes=[rss])
                    fw.op("dve", lambda: V.tensor_scalar(out=ss[:], in0=ss[:], scalar1=1.0 / HD, scalar2=EPS, op0=ALU.mult, op1=ALU.add),
                          reads=[rss], writes=[rss])
                    fw.op("act", lambda: A.activation(out=ss[:], in_=ss[:], func=AF.Sqrt), reads=[rss], writes=[rss])
                    fw.op("dve", lambda: V.reciprocal(out=ss[:], in_=ss[:]), reads=[rss], writes=[rss])
                    fw.op("dve", lambda: V.tensor_tensor(out=qk[:], in0=qk[:], in1=bc(ss[:], [128, 10, HD]), op=ALU.mult),
                          reads=[rqk, rss], writes=[rqk])
                    qb, rqb = qbR.next()
                    if isctx:
                        fw.op("dve", lambda: V.tensor_tensor(out=qb[:], in0=qk[:], in1=qkg[:], op=ALU.mult),
                              reads=[rqk, rP], writes=[rqb])
                    else:
                        fw.op("dve", lambda: V.tensor_tensor(out=qk[:], in0=qk[:], in1=qkg[:], op=ALU.mult),
                              reads=[rqk, rP], writes=[rqk])
                        x0 = qk[:].rearrange("p h (i two) -> p h i two", two=2)[:, :, :, 0]
                        x1 = qk[:].rearrange("p h (i two) -> p h i two", two=2)[:, :, :, 1]
                        o0 = qb[:].rearrange("p h (i two) -> p h i two", two=2)[:, :, :, 0]
                        o1 = qb[:].rearrange("p h (i two) -> p h i two", two=2)[:, :, :, 1]
                        rop, rrop = ropR.next()
                        ld(rop[:, 0, :], I["rope_c"][ti * 128:(ti + 1) * 128, :], writes=[rrop])
                        ld(rop[:, 1, :], I["rope_s"][ti * 128:(ti + 1) * 128, :], writes=[rrop])
                        cb_ = rop[:, 0, :].unsqueeze(1).to_broadcast([128, 10, 32])
                        sb_ = rop[:, 1, :].unsqueeze(1).to_broadcast([128, 10, 32])
                        (t1, r1), (t2, r2), (t3, r3), (t4, r4) = rtR.next(), rtR.next(), rtR.next(), rtR.next()
                        fw.op("dve", lambda: V.tensor_tensor(out=t1[:], in0=x0, in1=cb_, op=ALU.mult), reads=[rqk, rrop], writes=[r1])
                        fw.op("pool", lambda: G.tensor_tensor(out=t2[:], in0=x1, in1=sb_, op=ALU.mult), reads=[rqk, rrop], writes=[r2])
                        fw.op("dve", lambda: V.tensor_tensor(out=t3[:], in0=x0, in1=sb_, op=ALU.mult), reads=[rqk, rrop], writes=[r3])
                        fw.op("pool", lambda: G.tensor_tensor(out=t4[:], in0=x1, in1=cb_, op=ALU.mult), reads=[rqk, rrop], writes=[r4])
                        fw.op("dve", lambda: V.tensor_tensor(out=o0, in0=t1[:], in1=t2[:], op=ALU.subtract), reads=[r1, r2], writes=[rqb])
                        fw.op("dve", lambda: V.tensor_tensor(out=o1, in0=t3[:], in1=t4[:], op=ALU.add), reads=[r3, r4], writes=[rqb])
                    pt, rpt = PS.next()
                    ptb = pt[:].bitcast(BF16)
                    for hh in range(NH):
                        fw.op("pe", lambda hh=hh: PE.transpose(ptb[:HD, hh * 128:(hh + 1) * 128], qb[:, hh, :], L.ident_b[:]),
                              reads=[rqb, rC], writes=[rpt])
                    fw.op("dve", lambda: V.tensor_copy(out=qTs[:, :, s * 128:(s + 1) * 128],
                                                       in_=ptb[:HD, 0:1024].rearrange("p (h t) -> p h t", t=128)),
                          reads=[rpt], writes=[rqT])
                    pt2, rpt2 = PS.next()
                    ptb2 = pt2[:].bitcast(BF16)
                    for kk in range(NKV):
                        fw.op("pe", lambda kk=kk: PE.transpose(ptb2[:HD, kk * 128:(kk + 1) * 128], qb[:, 8 + kk, :], L.ident_b[:]),
                              reads=[rqb, rC], writes=[rpt2])
                    fw.op("dve", lambda: V.tensor_copy(out=KT[:HD, :, ti * 128:(ti + 1) * 128],
                                                       in_=ptb2[:HD, 0:256].rearrange("p (h t) -> p h t", t=128)),
                          reads=[rpt2], writes=[rKT])
                ld(L.QT[:, :, t0:t0 + nt], qTs[:, :, :nt], reads=[rqT], writes=[L.rQT])

        with Phase(fw) as e2:
            L.alloc_tail(e2)
            sb2 = lambda name, shape, dt=F32: e2.enter_context(nc.sbuf_tensor(_u(name), list(shape), dt))
            cvw = sb2("cvw", [128, 4, CW]); cvp = sb2("cvp", [128, 3, 4])
            dg = sb2("dg", [128, 4, CW, 128], BF16); rdg = Res("dg")
            ld(cvw[:], I["conv_w"][:, j], writes=[rP]); ld(cvp[:], I["conv_p"][:, j], writes=[rP])
            for cc in range(4):
                for k in range(CW):
                    fw.op("dve" if (k % 2) else "pool",
                          lambda cc=cc, k=k: (V if (k % 2) else G).tensor_scalar(
                              out=dg[:, cc, k, :], in0=L.ident_f[:], scalar1=cvw[:, cc, k:k + 1], scalar2=None, op0=ALU.mult),
                          reads=[rP, rC], writes=[rdg])
            qR = Ring(nc, e2, "Qb", [128, NH, 256], BF16, 2)
            for (qt_, rq_) in qR.items:
                fw.op("pool", lambda qt_=qt_: G.memset(qt_[HD:128, :, :], 0.0), writes=[rq_])
            pR = Ring(nc, e2, "PT", [128, 256], BF16, 6)
            atR = Ring(nc, e2, "att", [128, 2, 512], BF16, 2)
            rcR = Ring(nc, e2, "rec", [128, 2], F32, 2)
            mxR = Ring(nc, e2, "mixT", [128, KC, 256], BF16, 2)
            ubR = Ring(nc, e2, "ub", [128, 4, 256 + CW - 1], BF16, 2)
            for (t0, nt, isctx) in self.blocks:
                ns = nt // 128
                xin, rxin = L.load_x(t0, nt)
                Qb, rQ = qR.next()
                ld(Qb[:HD, :, :nt], L.QT[:, :, t0:t0 + nt], reads=[L.rQT], writes=[rQ])
                ub, rub = ubR.next()
                uc = (SEQ + 48 + (t0 - SEQ)) if isctx else (16 + t0)
                ld(ub[:, :, :nt + CW - 1], L.UT[:, :, uc - 15:uc + nt + 15], reads=[L.rUT], writes=[rub])
                kts = list(range(SEQ // 128, NT)) if isctx else list(range(NT))
                att, ratt = atR.next()
                for h in range(NH):
                    kv = h // (NH // NKV)
                    po, rpo = L.PO.next()
                    pend = []

                    def emit_pv(item):
                        pT, rpT, ki, kt = item
                        for s in range(ns):
                            f = lambda s=s: PE.matmul(po[:, s * 65:(s + 1) * 65], lhsT=pT[:, s * 128:(s + 1) * 128],
                                                      rhs=Vt[:, kt, kv, :], start=(ki == 0 and s == 0),
                                                      stop=(ki == len(kts) - 1), skip_group_check=True)
                            if s == ns - 1:
                                fw.op("pe", f, reads=[rpT, rVt], writes=[rpo])
                            else:
                                fw.op_nt("pe", f, reads=[rpT, rVt], writes=[rpo])

                    for ki, kt in enumerate(kts):
                        pS, rpS = PS.next()
                        fw.op("pe", lambda: PE.matmul(pS[:, :nt], lhsT=KT[:, kv, kt * 128:(kt + 1) * 128], rhs=Qb[:, h, :nt],
                                                      start=True, stop=True), reads=[rKT, rQ], writes=[rpS])
                        pT, rpT = pR.next()
                        fw.op("act", lambda: A.activation(out=pT[:, :nt], in_=pS[:, :nt], func=AF.Exp, scale=HD ** -0.5),
                              reads=[rpS], writes=[rpT])
                        pend.append((pT, rpT, ki, kt))
                        if len(pend) > 2:
                            emit_pv(pend.pop(0))
                    while pend:
                        emit_pv(pend.pop(0))
                    rec, rrec = rcR.next()
                    pov = po[:, 0:ns * 65].rearrange("p (s e) -> p s e", e=65)
                    fw.op("dve", lambda: V.reciprocal(out=rec[:, :ns], in_=pov[:, :, 64]), reads=[rpo], writes=[rrec])
                    fw.op("dve", lambda: V.tensor_tensor(out=att[:, :ns, h * HD:(h + 1) * HD], in0=pov[:, :, 0:HD],
                                                         in1=bc(rec[:, :ns], [128, ns, HD]), op=ALU.mult),
                          reads=[rpo, rrec], writes=[ratt])
                mixT, rmx = mxR.next()
                for s in range(ns):
                    pt, rpt = PS.next()
                    ptb = pt[:].bitcast(BF16)
                    for c_ in range(4):
                        fw.op("pe", lambda c_=c_: PE.transpose(ptb[:, c_ * 128:(c_ + 1) * 128], att[:, s, c_ * 128:(c_ + 1) * 128],
                                                               L.ident_b[:]), reads=[ratt, rC], writes=[rpt])
                    fw.op("dve", lambda: V.tensor_copy(out=mixT[:, 0:4, s * 128:(s + 1) * 128],
                                                       in_=ptb[:, 0:512].rearrange("p (c t) -> p c t", t=128)),
                          reads=[rpt], writes=[rmx])
                cvt, rcv = L.TR['zR'].next()
                for cc in range(4):
                    pc, rpc = PS.next()
                    mm_group(pc[:, :nt], [(dg[:, cc, k, :], ub[:, cc, k:k + nt]) for k in range(CW)], reads=[rdg, rub], writes=[rpc])
                    fw.op("act", lambda: A.activation(out=cvt[:, cc, :nt], in_=pc[:, :nt], func=AF.Identity, bias=cvp[:, 0, cc:cc + 1]),
                          reads=[rpc, rP], writes=[rcv])
                L.ln_fm(cvt, rcv, 4, nt, L.ones_c)
                for cc in range(4):
                    fw.op("act", lambda cc=cc: A.activation(out=mixT[:, 4 + cc, :nt], in_=cvt[:, cc, :nt], func=AF.Silu,
                                                           bias=cvp[:, 2, cc:cc + 1], scale=cvp[:, 1, cc:cc + 1]),
                          reads=[rcv, rP], writes=[rmx])

                def ych(dc):
                    py, rpy = PS.next()
                    mm_group(py[:, :nt], [(L.wout[:, kc, dc * 128:(dc + 1) * 128], mixT[:, kc, :nt]) for kc in range(KC)],
                             reads=[L.rwout, rmx], writes=[rpy])
                    return py[:, :nt], rpy
                L.mixer_tail(l, ych, t0, nt, isctx, xin, rxin)


K.even_layer = even_layer


def odd_layer(self, l, j, Ld):
    L = _NS(Ld)
    nc, fw, I, PS, V, A, PE, G = L.nc, L.fw, L.I, L.PS, L.V, L.A, L.PE, L.G
    mm_group, ld, ldc, rC = L.mm_group, L.ld, L.ldc, L.rC
    SEQ, T, NT, B = self.SEQ, self.T, self.NT, self.B
    Z, FN = L.Z, L.FN
    ldc(L.wout[:], I["fo_w"][j].rearrange("(c p) n -> p c n", p=128), writes=[L.rwout])
    ld(L.bo[:], I["fo_b"][:, j, :], reads=[L.rdv], writes=[rC])
    L.layer_vectors(l, L.bo[:])
    cp = [0]

    def evac(out, in_, reads, writes):
        cp[0] += 1
        if cp[0] % 2:
            fw.op("dve", lambda: V.tensor_copy(out=out, in_=in_), reads=reads, writes=writes)
        else:
            fw.op("act", lambda: A.copy(out=out, in_=in_), reads=reads, writes=writes)

    with Phase(fw) as o1:
        L.alloc_tail(o1, lite=True, depth=2)
        wc = o1.enter_context(nc.sbuf_tensor(_u("wc"), [128, 2, 512], BF16)); rwc = Res("wc")
        ld(wc[:], I["wc"], writes=[rwc])
        zR_ = Ring(nc, o1, "zrow", [128, 2, 2048], BF16, 2)
        for (t0, nt, isctx) in self.blocks:
            xin, rxin = L.load_x(t0, nt)
            hT, rh = L.mod1(l, xin, rxin, nt, isctx)
            zt, rzt = zR_.next()
            for s in range(nt // 128):
                for g in range(NG):
                    pz, rpz = PS.next()
                    mm_group(pz[:, :], [(hT[:, 2 * g + k2, s * 128:(s + 1) * 128], wc[:, k2, :]) for k2 in range(2)],
                             reads=[rh, rwc], writes=[rpz])
                    evac(zt[:, s, g * 512:(g + 1) * 512], pz[:, :], [rpz], [rzt])
            ld(Z[t0:t0 + nt, :].rearrange("(s p) n -> p s n", p=128), zt[:, :nt // 128, :], reads=[rzt], writes=[L.rZ])
    with Phase(fw) as o2:
        sb = lambda name, shape, dt=BF16: o2.enter_context(nc.sbuf_tensor(_u(name), list(shape), dt))
        t1, t2 = sb("t1", [128, 256]), sb("t2", [128, 256])
        mcos, msin = sb("mcos", [B, 128, B]), sb("msin", [B, 128, B])
        c256, s256 = sb("c256", [128, 2, 256]), sb("s256", [128, 2, 256])
        rT = Res("ffttab")
        for t_, nm in ((t1, "t1"), (t2, "t2"), (mcos, "mcos"), (msin, "msin"), (c256, "c256"), (s256, "s256")):
            ld(t_[:], I[nm], writes=[rT])
        zgR = Ring(nc, o2, "Zg", [128, B, 2, 128], BF16, 1)
        ybR = Ring(nc, o2, "Yb", [B, 64, 256], BF16, 2)
        fbR = Ring(nc, o2, "Fb", [B, 128, 64], BF16, 2)
        for g in range(NG):
            for cb in range(4):
                if cb % 2 == 0:
                    Zg, rZg = zgR.next()
                    hb = cb // 2
                    for ri in range(2):
                        ld(Zg[:, :, ri, :], Z[0:SEQ, g * 512 + ri * 256 + hb * 128:g * 512 + ri * 256 + (hb + 1) * 128]
                           .rearrange("(a b) n -> a b n", b=B), reads=[L.rZ], writes=[rZg])
                Yb, rYb = ybR.next()
                for i in range(32):
                    p1, rp1 = PS.next()
                    for e in range(2):
                        chl = (cb % 2) * 64 + 2 * i + e
                        mm_group(p1[:B, e * 256:(e + 1) * 256], [(Zg[:, :, 0, chl], t1[:]), (Zg[:, :, 1, chl], t2[:])],
                                 reads=[rZg, rT], writes=[rp1])
                    evac(Yb[:, 2 * i:2 * i + 2, :], p1[:B, :].rearrange("p (e c) -> p e c", c=256), [rp1], [rYb])
                Fb, rFb = fbR.next()
                for c0 in range(0, 128, 8):
                    p2, rp2 = PS.next()
                    for ci in range(8):
                        c_ = c0 + ci
                        mm_group(p2[:B, ci * 64:(ci + 1) * 64], [(mcos[:, c_, :], Yb[:, :, c_]), (msin[:, c_, :], Yb[:, :, 128 + c_])],
                                 reads=[rYb, rT], writes=[rp2])
                    evac(Fb[:, c0:c0 + 8, :], p2[:B, :].rearrange("p (c n) -> p c n", n=64), [rp2], [rFb])
                ld(FN[0:SEQ, g * 256 + cb * 64:g * 256 + (cb + 1) * 64].rearrange("(d c) n -> d c n", c=128), Fb[:],
                   reads=[rFb], writes=[L.rFN])
        Zc = sb("Zc", [128, 2, 2048]); rZc = Res("Zc")
        Fc = sb("Fc", [128, 2, 1024]); rFc = Res("Fc")
        ld(Zc[:], Z[SEQ:SEQ + CTX, :].rearrange("(c p) n -> p c n", p=128), reads=[L.rZ], writes=[rZc])
        for g in range(NG):
            for ko in range(2):
                pc, rpc = PS.next()
                pairs = []
                for n_ in range(2):
                    pairs.append((c256[:, n_, ko * 128:(ko + 1) * 128], Zc[:, n_, g * 512:g * 512 + 256]))
                    pairs.append((s256[:, n_, ko * 128:(ko + 1) * 128], Zc[:, n_, g * 512 + 256:g * 512 + 512]))
                mm_group(pc[:, 0:256], pairs, reads=[rZc, rT], writes=[rpc])
                evac(Fc[:, ko, g * 256:(g + 1) * 256], pc[:, 0:256], [rpc], [rFc])
        ld(FN[SEQ:SEQ + CTX, :].rearrange("(c p) n -> p c n", p=128), Fc[:], reads=[rFc], writes=[L.rFN])
    with Phase(fw) as o3:
        L.alloc_tail(o3, depth=2)
        frR = Ring(nc, o3, "frow", [128, 2, 1024], BF16, 2)
        ftR = Ring(nc, o3, "FT", [128, KC, 256], BF16, 2)
        for (t0, nt, isctx) in self.blocks:
            xin, rxin = L.load_x(t0, nt)
            fr, rfr = frR.next()
            ld(fr[:, :nt // 128, :], FN[t0:t0 + nt, :].rearrange("(s p) n -> p s n", p=128), reads=[L.rFN], writes=[rfr])
            FT, rFT = ftR.next()
            for s in range(nt // 128):
                pt, rpt = PS.next()
                ptb = pt[:].bitcast(BF16)
                for c_ in range(KC):
                    fw.op("pe", lambda c_=c_: PE.transpose(ptb[:, c_ * 128:(c_ + 1) * 128], fr[:, s, c_ * 128:(c_ + 1) * 128],
                                                           L.ident_b[:]), reads=[rfr, rC], writes=[rpt])
                evac(FT[:, :, s * 128:(s + 1) * 128], ptb[:, 0:1024].rearrange("p (c t) -> p c t", t=128), [rpt], [rFT])

            def ych(dc):
                py, rpy = PS.next()
                mm_group(py[:, :nt], [(L.wout[:, kc, dc * 128:(dc + 1) * 128], FT[:, kc, :nt]) for kc in range(KC)],
                         reads=[L.rwout, rFT], writes=[rpy])
                return py[:, :nt], rpy
            L.mixer_tail(l, ych, t0, nt, isctx, xin, rxin)


K.odd_layer = odd_layer


def moe_layer(self, l, Ld):
    L = _NS(Ld)
    nc, fw, I, PS, V, A, PE, G = L.nc, L.fw, L.I, L.PS, L.V, L.A, L.PE, L.G
    mm_group, ld, ldc, rC = L.mm_group, L.ld, L.ldc, L.rC
    SEQ, T, NT, FF, FC = self.SEQ, self.T, self.NT, self.FF, self.FC
    CAP, CAPC, NS, RW = self.CAP, self.CAPC, self.NS, self.RW
    NSP = ((NS + 127) // 128) * 128
    NST = NSP // 128
    HM, XG, Y, LG = L.HM, L.XG, L.Y, L.LG
    ITER = 30
    with Phase(fw) as m:
        sb = lambda name, shape, dt=F32: m.enter_context(nc.sbuf_tensor(_u(name), list(shape), dt))
        slot_i2 = sb("slot_i", [128, NT * NE], I32); rsl = Res("slot_i")
        slot_i = slot_i2[:].rearrange("p (i e) -> p i e", e=NE)
        meta = sb("meta", [128, NT, 1 + NE], I32); rme = Res("meta")
        with Phase(fw) as m1:
            sb1 = lambda name, shape, dt=F32: m1.enter_context(nc.sbuf_tensor(_u(name), list(shape), dt))
            aff = sb1("aff", [NE, T]); raf = Res("aff")
            msk = sb1("msk", [NE, T]); rmk = Res("msk")
            cs = sb1("cs", [NE, T]); rcs = Res("cs")
            stt = sb1("stt", [NE, 16]); rs_ = Res("stt")
            ld(aff[:], LG[:, :], reads=[L.rlog], writes=[raf])
            fw.op("act", lambda: A.activation(out=aff[:], in_=aff[:], func=AF.Exp), reads=[raf], writes=[raf])
            for b0 in range(0, T, 512):
                nb = min(512, T - b0)
                pp, rpp = PS.next()
                fw.op("pe", lambda: PE.matmul(pp[:NE, :nb], lhsT=L.ones16[:], rhs=aff[:, b0:b0 + nb], start=True, stop=True),
                      reads=[raf, rC], writes=[rpp])
                fw.op("dve", lambda: V.reciprocal(out=msk[:, b0:b0 + nb], in_=pp[:NE, :nb]), reads=[rpp], writes=[rmk])
            fw.op("dve", lambda: V.tensor_tensor(out=aff[:], in0=aff[:], in1=msk[:], op=ALU.mult), reads=[raf, rmk], writes=[raf])
            for c_, v in ((0, 0.0), (1, 0.0), (2, 1.5), (3, 1.5), (4, 0.75), (5, 0.75), (8, float(CAP)), (9, float(CAPC))):
                fw.op("dve", lambda c_=c_, v=v: V.memset(stt[:, c_:c_ + 1], v), writes=[rs_])
            rng = ((0, SEQ), (SEQ, T))
            for it in range(ITER):
                for q, (a0, a1) in enumerate(rng):
                    fw.op("dve", lambda q=q, a0=a0, a1=a1: V.tensor_scalar(
                        out=msk[:, a0:a1], in0=aff[:, a0:a1], scalar1=stt[:, 4 + q:5 + q], scalar2=None,
                        op0=ALU.is_ge, op1=ALU.add, accum_out=stt[:, 6 + q:7 + q]), reads=[raf, rs_], writes=[rmk, rs_])
                tt = lambda o, a, b_, op: fw.op("dve", lambda: V.tensor_tensor(out=stt[:, o:o + 2], in0=stt[:, a:a + 2],
                                                                               in1=stt[:, b_:b_ + 2], op=op), reads=[rs_], writes=[rs_])
                tt(10, 6, 8, ALU.is_ge)
                tt(12, 4, 0, ALU.subtract)
                tt(12, 12, 10, ALU.mult)
                tt(0, 0, 12, ALU.add)
                tt(12, 2, 4, ALU.subtract)
                tt(12, 12, 10, ALU.mult)
                tt(2, 4, 12, ALU.add)
                tt(4, 0, 2, ALU.add)
                fw.op("dve", lambda: V.tensor_scalar(out=stt[:, 4:6], in0=stt[:, 4:6], scalar1=0.5, scalar2=None, op0=ALU.mult),
                      reads=[rs_], writes=[rs_])
            for q, (a0, a1) in enumerate(rng):
                cap = float(CAP if q == 0 else CAPC)
                off = 0.0 if q == 0 else float(CAP)
                fw.op("dve", lambda: V.tensor_scalar(out=msk[:, a0:a1], in0=aff[:, a0:a1], scalar1=stt[:, q:q + 1], scalar2=None,
                                                     op0=ALU.is_ge), reads=[raf, rs_], writes=[rmk])
                fw.op("dve", lambda: V.tensor_tensor_scan(out=cs[:, a0:a1], data0=msk[:, a0:a1], data1=msk[:, a0:a1], initial=0.0,
                                                          op0=ALU.add, op1=ALU.max), reads=[rmk], writes=[rcs])
                fw.op("dve", lambda: V.scalar_tensor_tensor(out=msk[:, a0:a1], in0=cs[:, a0:a1], scalar=cap, in1=msk[:, a0:a1],
                                                            op0=ALU.is_le, op1=ALU.mult), reads=[rcs, rmk], writes=[rmk])
                fw.op("dve", lambda: V.scalar_tensor_tensor(out=cs[:, a0:a1], in0=cs[:, a0:a1], scalar=off - 1.0 - BIG,
                                                            in1=msk[:, a0:a1], op0=ALU.add, op1=ALU.mult), reads=[rcs, rmk], writes=[rcs])
                fw.op("dve", lambda: V.tensor_scalar(out=cs[:, a0:a1], in0=cs[:, a0:a1], scalar1=BIG, scalar2=None, op0=ALU.add),
                      reads=[rcs], writes=[rcs])
            fw.op("dve", lambda: V.tensor_copy(out=meta[:, :, 0], in_=L.tokid[:]), reads=[rC], writes=[rme])
            for g0 in range(0, NT, 16):
                gn = min(16, NT - g0)
                pt, rpt = PS.next()
                for i in range(gn):
                    ti = g0 + i
                    fw.op("pe", lambda: PE.transpose(pt[:, i * 32:i * 32 + 16], cs[:, ti * 128:(ti + 1) * 128], L.ident_f[:NE, :NE]),
                          reads=[rcs, rC], writes=[rpt])
                    fw.op("pe", lambda: PE.transpose(pt[:, i * 32 + 16:i * 32 + 32], aff[:, ti * 128:(ti + 1) * 128], L.ident_f[:NE, :NE]),
                          reads=[raf, rC], writes=[rpt])
                pv = pt[:, 0:gn * 32].rearrange("p (i k) -> p i k", k=32)
                fw.op("dve", lambda: V.tensor_copy(out=slot_i[:, g0:g0 + gn, :], in_=pv[:, :, 0:16]), reads=[rpt], writes=[rsl])
                fw.op("dve", lambda: V.tensor_copy(out=meta[:, g0:g0 + gn, 1:1 + NE].bitcast(F32), in_=pv[:, :, 16:32]),
                      reads=[rpt], writes=[rme])
            ld(HM[:, 1024:RW].rearrange("(i p) n -> p i n", p=128), meta[:].bitcast(BF16), reads=[rme], writes=[L.rHM])
        zr = L.zeros[:].rearrange("p (i n) -> p i n", n=1024)
        Yv = Y.rearrange("(i p) n -> p i n", p=128)
        for i0 in range(NT):
            ld(Yv[:, i0:i0 + 1, :], zr[:, 0:1, :], reads=[rC], writes=[L.rY])
        with Phase(fw) as m3:
            sb3 = lambda name, shape, dt=BF16: m3.enter_context(nc.sbuf_tensor(_u(name), list(shape), dt))
            xgR = Ring(nc, m3, "xg", [128, NST * RW], BF16, 1)
            XgT = sb3("XgT", [128, KC, NSP]); rXT = Res("XgT")
            hid = sb3("hid", [128, FC, NSP]); rhid = Res("hid")
            wgR = Ring(nc, m3, "wg", [128, KC, 256], BF16, 2)
            wuR = Ring(nc, m3, "wu", [128, KC, 256], BF16, 2)
            wdR = Ring(nc, m3, "wd", [128, FC, 1024], BF16, 1)
            slR = Ring(nc, m3, "sil", [128, 512], BF16, 2)
            yoR = Ring(nc, m3, "yo", [128, 1024], F32, 2)
            sblocks = [(s0, min(512, NS - s0)) for s0 in range(0, NS, 512)]
            rowR = Ring(nc, m3, "row", [128, RW], BF16, 3)

            def dispatch_gen(e_):
                for ti in range(NT):
                    rt_, rrt = rowR.next()
                    ld(rt_[:], HM[ti * 128:(ti + 1) * 128, :], reads=[L.rHM], writes=[rrt])
                    fw.dma("pool", lambda: G.indirect_dma_start(
                        out=XG[e_][:, :], out_offset=bass.IndirectOffsetOnAxis(ap=slot_i2[:, ti * NE + e_:ti * NE + e_ + 1], axis=0),
                        in_=rt_[:], in_offset=None, bounds_check=self.breg, oob_is_err=False),
                        reads=[rrt, rsl], writes=[L.rXG[e_]])
                    yield

            for _ in dispatch_gen(0):
                pass
            nblk = (FC + 1) // 2
            per_blk = (NT + nblk - 1) // nblk
            for e in range(NE):
                dgen = dispatch_gen(e + 1) if e + 1 < NE else iter(())
                xg2, rxg = xgR.next()
                xg = xg2[:].rearrange("p (i n) -> p i n", n=RW)
                ld(xg, XG[e].rearrange("(i p) n -> p i n", p=128), reads=[L.rXG[e]], writes=[rxg])
                wd, rwd0 = wdR.next()
                fw.war("pool", rwd0)
                rwd = []
                for f0 in range(0, FC, 4):
                    fn_ = min(4, FC - f0)
                    rr_ = Res("wdp")
                    ldc(wd[:, f0:f0 + fn_, :], I["w_down"][l, e, f0 * 128:(f0 + fn_) * 128, :].rearrange("(c p) n -> p c n", p=128),
                        writes=[rr_])
                    rwd.append(rr_)
                for st_ in range(NST):
                    pt, rpt = PS.next()
                    ptb = pt[:].bitcast(BF16)
                    for c_ in range(KC):
                        fw.op("pe", lambda c_=c_: PE.transpose(ptb[:, c_ * 128:(c_ + 1) * 128], xg[:, st_, c_ * 128:(c_ + 1) * 128],
                                                               L.ident_b[:]), reads=[rxg, rC], writes=[rpt])
                    fw.op("dve", lambda: V.tensor_copy(out=XgT[:, :, st_ * 128:(st_ + 1) * 128],
                                                       in_=ptb[:, 0:1024].rearrange("p (c t) -> p c t", t=128)),
                          reads=[rpt], writes=[rXT])
                for f0 in range(0, FC, 2):
                    fn_ = min(2, FC - f0)
                    wg, rwg = wgR.next()
                    wu, rwu = wuR.next()
                    ldc(wg[:, :, :fn_ * 128], I["w_gate"][l, e, :, f0 * 128:(f0 + fn_) * 128].rearrange("(c p) n -> p c n", p=128), writes=[rwg])
                    ldc(wu[:, :, :fn_ * 128], I["w_up"][l, e, :, f0 * 128:(f0 + fn_) * 128].rearrange("(c p) n -> p c n", p=128), writes=[rwu])
                    for _k in range(per_blk):
                        next(dgen, None)
                    for fi in range(fn_):
                        for (s0, sn) in sblocks:
                            pg, rpg = PS.next()
                            mm_group(pg[:, :sn], [(wg[:, kc, fi * 128:(fi + 1) * 128], XgT[:, kc, s0:s0 + sn]) for kc in range(KC)],
                                     reads=[rwg, rXT], writes=[rpg])
                            pu, rpu = PS.next()
                            mm_group(pu[:, :sn], [(wu[:, kc, fi * 128:(fi + 1) * 128], XgT[:, kc, s0:s0 + sn]) for kc in range(KC)],
                                     reads=[rwu, rXT], writes=[rpu])
                            sl, rsl_ = slR.next()
                            fw.op("act", lambda: A.activation(out=sl[:, :sn], in_=pg[:, :sn], func=AF.Silu), reads=[rpg], writes=[rsl_])
                            fw.op("dve", lambda: V.tensor_tensor(out=hid[:, f0 + fi, s0:s0 + sn], in0=sl[:, :sn], in1=pu[:, :sn], op=ALU.mult),
                                  reads=[rsl_, rpu], writes=[rhid])
                for _ in dgen:
                    pass
                for st_ in range(NST):
                    rows = min(128, NS - st_ * 128)
                    if rows <= 0:
                        continue
                    yo, ryo = yoR.next()
                    for half in range(2):
                        py, rpy = PS.next()
                        mm_group(py[:rows, :], [(hid[:, f, st_ * 128:st_ * 128 + rows], wd[:, f, half * 512:(half + 1) * 512]) for f in range(FC)],
                                 reads=[rhid, rwd0] + rwd, writes=[rpy])
                        gate = xg2[:rows, st_ * RW + 1026 + 2 * e:st_ * RW + 1028 + 2 * e].bitcast(F32)
                        fw.op("act" if half else "dve",
                              (lambda: A.activation(out=yo[:rows, 512:1024], in_=py[:rows, :], func=AF.Identity, scale=gate)) if half else
                              (lambda: V.tensor_scalar(out=yo[:rows, 0:512], in0=py[:rows, :], scalar1=gate, scalar2=None, op0=ALU.mult)),
                              reads=[rpy, rxg], writes=[ryo])
                    idx = xg2[:rows, st_ * RW + 1024:st_ * RW + 1026].bitcast(I32)
                    fw.dma("pool", lambda: G.indirect_dma_start(
                        out=Y[:, :], out_offset=bass.IndirectOffsetOnAxis(ap=idx, axis=0), in_=yo[:rows, :], in_offset=None,
                        compute_op=ALU.add), reads=[ryo, rxg], writes=[L.rY])
    with Phase(fw) as m4:
        L.alloc_tail(m4, depth=2)
        yrR = Ring(nc, m4, "yrow", [128, 2, 1024], F32, 2)
        lng, lnb, modv = L.lng, L.lnb, L.modv
        for (t0, nt, isctx) in self.blocks:
            col = 1 if isctx else 0
            xin, rxin = L.load_x2(t0, nt)
            yr, ryr = yrR.next()
            ld(yr[:, :nt // 128, :], Y[t0:t0 + nt, :].rearrange("(s p) n -> p s n", p=128), reads=[L.rY], writes=[ryr])
            zt, rz = L.TR['zR'].next()
            for s in range(nt // 128):
                for half in range(2):
                    pt, rpt = PS.next()
                    for c4 in range(4):
                        c_ = half * 4 + c4
                        fw.op("pe", lambda c_=c_, c4=c4: PE.transpose(pt[:, c4 * 128:(c4 + 1) * 128], yr[:, s, c_ * 128:(c_ + 1) * 128],
                                                                      L.ident_f[:]), reads=[ryr, rC], writes=[rpt])
                    for c4 in range(4):
                        c_ = half * 4 + c4
                        fw.op("act", lambda c_=c_, c4=c4: A.activation(out=zt[:, c_, s * 128:(s + 1) * 128], in_=pt[:, c4 * 128:(c4 + 1) * 128],
                                                                      func=AF.Identity, scale=modv[:, l, 5, c_, col:col + 1]),
                              reads=[rpt, L.rmod_], writes=[rz])
            for c_ in range(KC):
                fw.op("dve", lambda c_=c_: V.scalar_tensor_tensor(out=zt[:, c_, :nt], in0=xin[:, c_, :nt], scalar=self.alpha,
                                                                 in1=zt[:, c_, :nt], op0=ALU.mult, op1=ALU.add),
                      reads=[rxin, rz], writes=[rz])
            L.ln_fm(zt, rz, KC, nt, L.ones_d)
            xo, rxo = L.TR['xoR'].next()
            for c_ in range(KC):
                fw.op("act", lambda c_=c_: A.activation(out=xo[:, c_, :nt], in_=zt[:, c_, :nt], func=AF.Identity,
                                                       bias=lnb[:, l, 1, c_:c_ + 1], scale=lng[:, l, 1, c_:c_ + 1]),
                      reads=[rz, rC], writes=[rxo])
            ld(L.XR[:, :, t0:t0 + nt], xo[:, :, :nt], reads=[rxo], writes=[L.rXR])


K.moe_layer = moe_layer


def _fm(v):
    v = np.asarray(v, np.float32)
    lead = v.shape[:-1]
    return np.ascontiguousarray(np.moveaxis(v.reshape(lead + (KC, 128)), -1, 0))


def make_inputs(kk, b, inp):
    SEQ, DEPTH = kk.SEQ, kk.DEPTH
    NEV, NOD = (DEPTH + 1) // 2, DEPTH // 2
    f32 = lambda a: np.ascontiguousarray(np.asarray(a, np.float32))
    m = dict(kk.host_consts())
    xa = np.concatenate([inp["x"][b], inp["ctx"][b]], axis=0)
    m["xT"] = f32(xa.T.reshape(KC, 128, kk.T).transpose(1, 0, 2))
    cc = np.stack([inp["c"][b], inp["c_ctx"], inp["c"][b], inp["c_ctx"]], axis=-1)
    m["cT"] = f32(cc.reshape(KC, 128, 4).transpose(1, 0, 2))
    m["ada_w"] = f32(inp["ada_w"])
    m["ada_b"] = f32(np.asarray(inp["ada_b"]).reshape(DEPTH, 48, 128).transpose(2, 0, 1))
    m["ln_g"] = f32(_fm(inp["ln_g"]))
    m["ln_b"] = f32(_fm(inp["ln_b"]))
    m["in_w"] = f32(inp["attn_in_w"])
    ib = np.asarray(inp["attn_in_b"], np.float32)
    m["in_b_tm"] = f32(np.broadcast_to(ib[None, :, 0:768], (128, NEV, 768)))
    m["in_b_fm"] = f32(ib[:, 768:].reshape(NEV, 8, 128).transpose(2, 0, 1))
    qg = np.asarray(inp["q_norm_g"], np.float32)
    kg = np.asarray(inp["k_norm_g"], np.float32)
    qk = np.concatenate([np.tile(qg, (1, NH)), np.tile(kg, (1, NKV))], axis=1)
    m["qk_g"] = f32(np.broadcast_to(qk[None], (128, NEV, 640)))
    cw = np.asarray(inp["conv_w"], np.float32)
    m["conv_w"] = f32(cw.reshape(NEV, CW, 4, 128).transpose(3, 0, 2, 1))
    cp = np.stack([inp["conv_b"], inp["conv_ln_g"], inp["conv_ln_b"]], axis=1)
    m["conv_p"] = f32(np.asarray(cp, np.float32).reshape(NEV, 3, 4, 128).transpose(3, 0, 1, 2))
    m["out_w"] = f32(inp["attn_out_w"])
    m["out_b"] = f32(_fm(inp["attn_out_b"]))
    if NOD:
        m["fo_w"] = f32(inp["fourier_out_w"])
        m["fo_b"] = f32(_fm(inp["fourier_out_b"]))
    m["r_w"] = f32(inp["router_w"])
    m["w_gate"] = f32(inp["expert_w_gate"])
    m["w_up"] = f32(inp["expert_w_up"])
    m["w_down"] = f32(inp["expert_w_down"])
    return m


def run(inp, SEQ, FF, DEPTH, trace=False, debug=False):
    kk = K(SEQ, FF, DEPTH)
    kk.debug = debug
    nc = kk.build()
    nb = inp["x"].shape[0]
    in_maps = [make_inputs(kk, b, inp) for b in range(nb)]
    res = run_bass_kernel_spmd(nc, in_maps, core_ids=list(range(nb)), trace=trace)
    outs = []
    for b in range(nb):
        yT = res.results[b]["yT"]
        outs.append(np.ascontiguousarray(yT.transpose(2, 1, 0).reshape(SEQ, D)))
    return np.stack(outs, 0).astype(np.float32), res


def kernel(**inputs):
    out, _ = run(inputs, 8192, 2816, 4)
    return out
```

```python
import numpy as np
from contextlib import ExitStack
import threading
import concourse.bass as bass
import concourse.mybir as mybir
from concourse.bass_utils import run_bass_kernel_spmd

F32 = mybir.dt.float32
BF16 = mybir.dt.bfloat16
I32 = mybir.dt.int32
AF = mybir.ActivationFunctionType
ALU = mybir.AluOpType
AX = mybir.AxisListType

D = 1024
KC = 8
CTX = 256
NE = 16
NH = 8
NKV = 2
HD = 64
CONVD = 512
CW = 31
NG = 4
EPS = 1e-6
BIG = 1.0e6


_UC = [0]


def _u(name):
    _UC[0] += 1
    return 'sb_%s_%d' % (name, _UC[0])


class Res:
    __slots__ = ("name", "w", "r")

    def __init__(self, name=""):
        self.name = name
        self.w = None
        self.r = []


class Co:
    def __init__(self, fw, fn):
        self.fw, self.fn = fw, fn
        self.done, self.started, self.err = False, False, None
        self.go, self.back = threading.Semaphore(0), threading.Semaphore(0)
        self.t = threading.Thread(target=self._run, daemon=True)

    def _run(self):
        self.go.acquire()
        try:
            self.fn()
        except BaseException as e:
            self.err = e
        finally:
            self.done = True
            self.back.release()

    def step(self):
        if self.done:
            return False
        if not self.started:
            self.started = True
            self.t.start()
        self.fw.co = self
        self.go.release()
        self.back.acquire()
        self.fw.co = None
        if self.err is not None:
            raise self.err
        return not self.done

    def drain(self):
        while self.step():
            pass

    def pause(self):
        self.back.release()
        self.go.acquire()


class FW:
    def __init__(self, nc, n_dma_sems=32):
        self.nc = nc
        self.engs = {"pe": nc.tensor, "act": nc.scalar, "dve": nc.vector, "pool": nc.gpsimd, "sp": nc.sync}
        self.sems, self.cnt, self._ctx = {}, {}, []
        for k in self.engs:
            cm = nc.semaphore("s_" + k)
            self.sems[k] = cm.__enter__()
            self._ctx.append(cm)
            self.cnt[k] = 0
        self.dma_sems = []
        for i in range(n_dma_sems):
            cm = nc.semaphore("d_%d" % i)
            self.sems["dma%d" % i] = cm.__enter__()
            self._ctx.append(cm)
            self.cnt["dma%d" % i] = 0
            self.dma_sems.append("dma%d" % i)
        self.dma_rr = 0
        self.waited = {k: {} for k in self.engs}
        self.n_inst = 0
        self.co = None

    def close(self):
        for cm in reversed(self._ctx):
            cm.__exit__(None, None, None)

    def _wait(self, eng, dep):
        if dep is None:
            return
        key, val = dep
        w = self.waited[eng]
        if w.get(key, 0) >= val:
            return
        if key == eng and eng == "pe":
            return
        self.engs[eng].wait_ge(self.sems[key], val)
        w[key] = val

    def deps(self, eng, reads, writes):
        for r in reads:
            self._wait(eng, r.w)
        for wr in writes:
            self._wait(eng, wr.w)
            for d in wr.r:
                self._wait(eng, d)

    def _done(self, tok, reads, writes):
        for r in reads:
            r.r.append(tok)
            if len(r.r) > 24:
                best = {}
                for kk, vv in r.r:
                    if best.get(kk, 0) < vv:
                        best[kk] = vv
                r.r = list(best.items())
        for wr in writes:
            wr.w = tok
            wr.r = []
        self.n_inst += 1

    def op(self, eng, fn, reads=(), writes=()):
        self.deps(eng, reads, writes)
        inst = fn()
        self.cnt[eng] += 1
        inst.then_inc(self.sems[eng], 1)
        tok = (eng, self.cnt[eng])
        self._done(tok, reads, writes)
        co = self.co
        if co is not None and threading.current_thread() is co.t:
            co.pause()
        return tok

    def op_nt(self, eng, fn, reads=(), writes=()):
        self.deps(eng, reads, writes)
        fn()
        self.n_inst += 1

    def dma(self, eng, fn, reads=(), writes=()):
        key = self.dma_sems[self.dma_rr]
        self.dma_rr = (self.dma_rr + 1) % len(self.dma_sems)
        if self.cnt[key] > 0:
            self._wait(eng, (key, self.cnt[key]))
        self.deps(eng, reads, writes)
        inst = fn()
        self.cnt[key] += 16
        inst.then_inc(self.sems[key], 16)
        tok = (key, self.cnt[key])
        self._done(tok, reads, writes)
        return tok

    def war(self, eng, res):
        self._wait(eng, res.w)
        for d in res.r:
            self._wait(eng, d)
        res.r = []

    def barrier(self):
        for eng in self.engs:
            for key, c in self.cnt.items():
                if c > 0:
                    self._wait(eng, (key, c))

    def drain(self, eng="sp"):
        for key, c in self.cnt.items():
            if c > 0:
                self._wait(eng, (key, c))


class Phase(ExitStack):
    def __init__(self, fw):
        super().__init__()
        self._fw = fw

    def __exit__(self, *a):
        self._fw.barrier()
        return super().__exit__(*a)


class Ring:
    def __init__(self, nc, es, name, shape, dt, n, psum=False):
        self.items = []
        for i in range(n):
            alloc = nc.psum_tensor if psum else nc.sbuf_tensor
            t = es.enter_context(alloc(_u("%s_%d" % (name, i)), shape, dt))
            self.items.append((t, Res("%s_%d" % (name, i))))
        self.i = 0
        self.lo, self.hi = 0, n

    def window(self, lo, hi):
        self.lo, self.hi = lo, hi
        self.i = lo

    def view(self, lo, hi):
        r = Ring.__new__(Ring)
        r.items = self.items[lo:hi]
        r.i, r.lo, r.hi = 0, 0, hi - lo
        return r

    def next(self):
        if not (self.lo <= self.i < self.hi):
            self.i = self.lo
        it = self.items[self.i]
        self.i = self.i + 1
        if self.i >= self.hi:
            self.i = self.lo
        return it


def _bf16(a):
    import ml_dtypes
    return np.ascontiguousarray(a).astype(ml_dtypes.bfloat16)


class K:
    def __init__(self, SEQ, FF, DEPTH):
        self.SEQ, self.FF, self.DEPTH = SEQ, FF, DEPTH
        self.T = SEQ + CTX
        self.NT = self.T // 128
        self.FC = FF // 128
        self.CAP = 2 * SEQ // NE
        self.CAPC = 2 * CTX // NE
        self.NS = self.CAP + self.CAPC
        self.B = SEQ // 128
        self.blocks = [(i * 256, 256, False) for i in range(SEQ // 256)] + [(SEQ, CTX, True)]
        self.alpha = (2 * DEPTH) ** 0.25

    def host_consts(self):
        SEQ, B = self.SEQ, self.B
        c = {}
        c["ident_f"] = np.eye(128, dtype=np.float32)
        c["ident_b"] = _bf16(np.eye(128))
        c["ones_d"] = np.full((128, 128), 1.0 / D, np.float32)
        c["ones_c"] = np.full((128, 128), 1.0 / CONVD, np.float32)
        c["ones16"] = np.ones((16, 16), np.float32)
        rows = SEQ // 64
        row = np.repeat(np.arange(rows, dtype=np.float32), 64)
        col = np.tile(np.arange(64, dtype=np.float32), rows)
        inv = (10000.0 ** (-np.arange(0, 32, 2, dtype=np.float32) / 32)).astype(np.float32)
        ang = np.concatenate([row[:, None] * inv, col[:, None] * inv], axis=-1).astype(np.float32)
        c["rope_c"] = np.cos(ang).astype(np.float32)
        c["rope_s"] = np.sin(ang).astype(np.float32)
        n = np.arange(256, dtype=np.float64)
        th = 2 * np.pi * np.outer(n, n) / 256.0
        wc = np.concatenate([np.cos(th), -np.sin(th)], axis=1) / 16.0
        c["wc"] = _bf16(wc.reshape(2, 128, 512).transpose(1, 0, 2))
        a = np.arange(128, dtype=np.float64)
        th = 2 * np.pi * np.outer(a, a) / 128.0
        c["t1"] = _bf16(np.concatenate([np.cos(th), -np.sin(th)], axis=1))
        c["t2"] = _bf16(np.concatenate([np.sin(th), np.cos(th)], axis=1))
        b = np.arange(B, dtype=np.float64)[:, None, None]
        cc = np.arange(128, dtype=np.float64)[None, :, None]
        d = np.arange(B, dtype=np.float64)[None, None, :]
        th = 2 * np.pi * b * (cc + 128 * d) / SEQ
        sc = 1.0 / np.sqrt(SEQ)
        c["mcos"] = _bf16(np.cos(th) * sc)
        c["msin"] = _bf16(np.sin(th) * sc)
        c["c256"] = _bf16((np.cos(2 * np.pi * np.outer(n, n) / 256) / 16.0).reshape(2, 128, 256).transpose(1, 0, 2))
        c["s256"] = _bf16((np.sin(2 * np.pi * np.outer(n, n) / 256) / 16.0).reshape(2, 128, 256).transpose(1, 0, 2))
        c["tokid"] = (np.arange(128)[:, None] + 128 * np.arange(self.NT)[None, :]).astype(np.int32)
        return c

    def build(self):
        SEQ, T, FF, DEPTH, NT, FC = self.SEQ, self.T, self.FF, self.DEPTH, self.NT, self.FC
        NEV, NOD = (DEPTH + 1) // 2, DEPTH // 2
        nc = bass.Bass("TRN2", target_bir_lowering=False)
        self.nc = nc
        fw = FW(nc)
        self.fw = fw
        es = ExitStack()

        def din(name, shape, dt=F32):
            return nc.dram_tensor(name, list(shape), dt, kind="ExternalInput").ap()

        def dscr(name, shape, dt=F32):
            if getattr(self, "debug", False):
                return nc.dram_tensor(name, list(shape), dt, kind="ExternalOutput").ap()
            return nc.dram_tensor(name, list(shape), dt).ap()

        I = {}
        hc = self.host_consts()
        for nm, arr in hc.items():
            dt = {np.dtype(np.float32): F32, np.dtype(np.int32): I32}.get(arr.dtype, BF16)
            I[nm] = din(nm, arr.shape, dt)
        I["xT"] = din("xT", [128, KC, T])
        I["cT"] = din("cT", [128, KC, 4])
        I["ada_w"] = din("ada_w", [DEPTH, D, 6 * D])
        I["ada_b"] = din("ada_b", [128, DEPTH, 48])
        I["ln_g"] = din("ln_g", [128, DEPTH, 2, KC])
        I["ln_b"] = din("ln_b", [128, DEPTH, 2, KC])
        I["in_w"] = din("in_w", [NEV, D, 1792])
        I["in_b_tm"] = din("in_b_tm", [128, NEV, 768])
        I["in_b_fm"] = din("in_b_fm", [128, NEV, 8])
        I["qk_g"] = din("qk_g", [128, NEV, 10 * HD])
        I["conv_w"] = din("conv_w", [128, NEV, 4, CW])
        I["conv_p"] = din("conv_p", [128, NEV, 3, 4])
        I["out_w"] = din("out_w", [NEV, D, D])
        I["out_b"] = din("out_b", [128, NEV, KC])
        if NOD:
            I["fo_w"] = din("fo_w", [NOD, D, D])
            I["fo_b"] = din("fo_b", [128, NOD, KC])
        I["r_w"] = din("r_w", [DEPTH, D, NE])
        I["w_gate"] = din("w_gate", [DEPTH, NE, D, FF])
        I["w_up"] = din("w_up", [DEPTH, NE, D, FF])
        I["w_down"] = din("w_down", [DEPTH, NE, FF, D])
        self.I = I
        OUT = nc.dram_tensor("yT", [128, KC, SEQ], F32, kind="ExternalOutput").ap()

        XR = dscr("XR", [128, KC, T]); rXR = Res("XR")
        QT = dscr("QT", [HD, NH, T], BF16); rQT = Res("QT")
        UT = dscr("UT", [128, 4, T + 64], BF16); rUT = Res("UT")
        Z = dscr("Z", [T, 2 * D], BF16); rZ = Res("Z")
        FN = dscr("FN", [T, D], BF16); rFN = Res("FN")
        RW = 1024 + 2 + 2 * NE
        self.RW = RW
        HM = dscr("HM", [T, RW], BF16); rHM = Res("HM")
        NSP = ((self.NS + 127) // 128) * 128
        XG = [dscr("XG%d" % e, [NSP, RW], BF16) for e in range(NE)]; rXG = [Res("XG%d" % e) for e in range(NE)]
        Y = dscr("Y", [T, D]); rY = Res("Y")

        sb = lambda name, shape, dt=F32: es.enter_context(nc.sbuf_tensor(_u(name), list(shape), dt))
        ident_f, ident_b = sb("ident_f", [128, 128]), sb("ident_b", [128, 128], BF16)
        ones_d, ones_c, ones16 = sb("ones_d", [128, 128]), sb("ones_c", [128, 128]), sb("ones16", [16, 16])
        tokid = sb("tokid", [128, NT], I32)
        modv = sb("modv", [128, DEPTH, 6, KC, 2])
        lng, lnb = sb("lng", [128, DEPTH, 2, KC]), sb("lnb", [128, DEPTH, 2, KC])
        zeros = sb("zeros", [128, 1024])
        rC = Res("consts")
        qi = [0]
        hwq = [("sp", nc.sync), ("act", nc.scalar)]

        def ld(out, in_, reads=(), writes=()):
            nm, q = hwq[qi[0] % 2]
            qi[0] += 1
            return fw.dma(nm, lambda: q.dma_start(out=out, in_=in_), reads=reads, writes=writes)

        def ldc(out, in_, reads=(), writes=()):
            return fw.dma("pool", lambda: nc.gpsimd.dma_start(out=out, in_=in_), reads=reads, writes=writes)

        for t, nm in ((ident_f, "ident_f"), (ident_b, "ident_b"), (ones_d, "ones_d"), (ones_c, "ones_c"),
                      (ones16, "ones16"), (tokid, "tokid"), (lng, "ln_g"), (lnb, "ln_b")):
            ld(t[:], I[nm], writes=[rC])
        fw.op("pool", lambda: nc.gpsimd.memset(zeros[:], 0.0), writes=[rC])

        PS = Ring(nc, es, "ps", [128, 512], F32, 6, psum=True)
        PO = Ring(nc, es, "po", [128, 512], F32, 2, psum=True)

        V = nc.vector
        A = nc.scalar
        PE = nc.tensor
        G = nc.gpsimd

        def mm_group(out, pairs, reads, writes, first=True, last=True):
            n = len(pairs)
            for i, (l, r) in enumerate(pairs):
                st, sp_ = (first and i == 0), (last and i == n - 1)
                f = lambda l=l, r=r, st=st, sp_=sp_: PE.matmul(out, lhsT=l, rhs=r, start=st, stop=sp_,
                                                              skip_group_check=True)
                if i == n - 1:
                    fw.op("pe", f, reads=reads, writes=writes)
                else:
                    fw.op_nt("pe", f, reads=reads, writes=writes)

        with Phase(fw) as ps_:
            cs = ps_.enter_context(nc.sbuf_tensor(_u("cs"), [128, KC, 4], F32)); rcs = Res("cs")
            adb = ps_.enter_context(nc.sbuf_tensor(_u("adb"), [128, DEPTH, 48], F32)); radb = Res("adb")
            wr = Ring(nc, ps_, "adw", [128, KC, 512], F32, 2)
            rmod = Res("modv")
            ld(cs[:], I["cT"], writes=[rcs])
            ld(adb[:], I["ada_b"], writes=[radb])
            fw.op("act", lambda: A.activation(out=cs[:], in_=cs[:], func=AF.Silu), reads=[rcs], writes=[rcs])
            for l in range(DEPTH):
                for cb in range(12):
                    wt, rw = wr.next()
                    ld(wt[:], I["ada_w"][l, :, cb * 512:(cb + 1) * 512].rearrange("(c p) n -> p c n", p=128), writes=[rw])
                    pt, rp = PS.next()
                    for j in range(4):
                        mm_group(pt[:, 4 * j:4 * j + 4],
                                 [(wt[:, kc, j * 128:(j + 1) * 128], cs[:, kc, :]) for kc in range(KC)],
                                 reads=[rw, rcs], writes=[rp])
                    for j in range(4):
                        jj = cb * 4 + j
                        six, ch = jj // 8, jj % 8
                        fw.op("dve", lambda j=j, jj=jj, six=six, ch=ch: V.tensor_scalar(
                            out=modv[:, l, six, ch, :], in0=pt[:, 4 * j:4 * j + 2], scalar1=adb[:, l, jj:jj + 1],
                            scalar2=(1.0 if six in (1, 4) else 0.0), op0=ALU.add, op1=ALU.add),
                            reads=[rp, radb], writes=[rmod])
            self.rmod = rmod
            if getattr(self, "debug", False):
                DBGM = dscr("DBGM", [128, DEPTH * 6 * KC * 2])
                ld(DBGM, modv[:].rearrange("p l s c t -> p (l s c t)"), reads=[rmod])

        def mv(l, six, ch, col):
            return modv[:, l, six, ch, col:col + 1]

        dv = sb("dv", [128, 8, KC, 2]); rdv = Res("dv")

        TR = {}
        tcnt = [0]

        def alloc_tail(es_, lite=False, depth=1):
            tcnt[0] += 1
            sfx = "_%d" % tcnt[0]
            TR["xinR"] = Ring(nc, es_, "xin" + sfx, [128, KC, 256], F32, 2)
            TR["hTR"] = Ring(nc, es_, "hT" + sfx, [128, KC, 256], BF16, depth)
            if lite:
                return
            TR["zR"] = Ring(nc, es_, "z" + sfx, [128, KC, 256], F32, 2)
            TR["sqR"] = Ring(nc, es_, "sq" + sfx, [128, KC, 256], F32, depth)
            TR["stR"] = Ring(nc, es_, "st" + sfx, [128, 2, 256], F32, depth)
            TR["xoR"] = Ring(nc, es_, "xo" + sfx, [128, KC, 256], F32, depth)
            TR["hmR"] = Ring(nc, es_, "hmT" + sfx, [128, KC, 256], BF16, depth)
            TR["hrR"] = Ring(nc, es_, "hrow" + sfx, [128, 2, 1024], BF16, depth)

        def ln_fm(zt, rz, nch, nt, ones_t):
            sq, rsq = TR['sqR'].next()
            for c_ in range(nch):
                fw.op("act", lambda c_=c_: A.activation(out=sq[:, c_, :nt], in_=zt[:, c_, :nt], func=AF.Square),
                      reads=[rz], writes=[rsq])
            pm, rpm = PS.next()
            pq, rpq = PS.next()
            mm_group(pm[:, :nt], [(ones_t[:], zt[:, c_, :nt]) for c_ in range(nch)], reads=[rz, rC], writes=[rpm])
            mm_group(pq[:, :nt], [(ones_t[:], sq[:, c_, :nt]) for c_ in range(nch)], reads=[rsq, rC], writes=[rpq])
            st, rst = TR['stR'].next()
            fw.op("dve", lambda: V.tensor_copy(out=st[:, 0, :nt], in_=pm[:, :nt]), reads=[rpm], writes=[rst])
            fw.op("dve", lambda: V.tensor_tensor(out=st[:, 1, :nt], in0=st[:, 0, :nt], in1=st[:, 0, :nt], op=ALU.mult),
                  reads=[rst], writes=[rst])
            fw.op("dve", lambda: V.tensor_tensor(out=st[:, 1, :nt], in0=pq[:, :nt], in1=st[:, 1, :nt], op=ALU.subtract),
                  reads=[rpq, rst], writes=[rst])
            fw.op("dve", lambda: V.tensor_scalar(out=st[:, 1, :nt], in0=st[:, 1, :nt], scalar1=0.0, scalar2=EPS,
                                                 op0=ALU.max, op1=ALU.add), reads=[rst], writes=[rst])
            fw.op("act", lambda: A.activation(out=st[:, 1, :nt], in_=st[:, 1, :nt], func=AF.Sqrt), reads=[rst], writes=[rst])
            fw.op("dve", lambda: V.reciprocal(out=st[:, 1, :nt], in_=st[:, 1, :nt]), reads=[rst], writes=[rst])
            for c_ in range(nch):
                fw.op("dve", lambda c_=c_: V.tensor_tensor(out=zt[:, c_, :nt], in0=zt[:, c_, :nt], in1=st[:, 0, :nt],
                                                         op=ALU.subtract), reads=[rz, rst], writes=[rz])
                fw.op("pool", lambda c_=c_: G.tensor_tensor(out=zt[:, c_, :nt], in0=zt[:, c_, :nt], in1=st[:, 1, :nt],
                                                          op=ALU.mult), reads=[rz, rst], writes=[rz])

        LG = dscr("LG", [NE, T]); rlog = Res("LG")
        lgR = Ring(nc, es, "lgt", [NE, 256], F32, 2)
        rw_b = sb("rw_b", [128, KC, NE], BF16); rrw = Res("rw")

        def mixer_tail(l, yps, t0, nt, isctx, xin, rxin, src=None):
            col = 1 if isctx else 0
            zt, rz = TR['zR'].next()
            for c_ in range(KC):
                pa, rp = yps(c_)
                fw.op("act", lambda c_=c_, pa=pa: A.activation(out=zt[:, c_, :nt], in_=pa, func=AF.Identity,
                                                              bias=dv[:, 0, c_, col:col + 1], scale=mv(l, 2, c_, col)),
                      reads=[rp, rdv, self.rmod], writes=[rz])
                fw.op("dve", lambda c_=c_: V.scalar_tensor_tensor(out=zt[:, c_, :nt], in0=xin[:, c_, :nt],
                                                                 scalar=self.alpha, in1=zt[:, c_, :nt],
                                                                 op0=ALU.mult, op1=ALU.add),
                      reads=[rxin, rz], writes=[rz])
            ln_fm(zt, rz, KC, nt, ones_d)
            xo, rxo = TR['xoR'].next()
            hm, rhm = TR['hmR'].next()
            for c_ in range(KC):
                fw.op("act", lambda c_=c_: A.activation(out=xo[:, c_, :nt], in_=zt[:, c_, :nt], func=AF.Identity,
                                                       bias=lnb[:, l, 0, c_:c_ + 1], scale=lng[:, l, 0, c_:c_ + 1]),
                      reads=[rz, rC], writes=[rxo])
                fw.op("act", lambda c_=c_: A.activation(out=hm[:, c_, :nt], in_=zt[:, c_, :nt], func=AF.Identity,
                                                       bias=dv[:, 2, c_, col:col + 1], scale=dv[:, 1, c_, col:col + 1]),
                      reads=[rz, rdv], writes=[rhm])
            ld(XR[:, :, t0:t0 + nt], xo[:, :, :nt], reads=[rxo], writes=[rXR])
            pl, rpl = PS.next()
            mm_group(pl[:NE, :nt], [(rw_b[:, kc, :], hm[:, kc, :nt]) for kc in range(KC)], reads=[rrw, rhm], writes=[rpl])
            lgt, rlgt = lgR.next()
            fw.op("dve", lambda: V.tensor_copy(out=lgt[:, :nt], in_=pl[:NE, :nt]), reads=[rpl], writes=[rlgt])
            ld(LG[:, t0:t0 + nt], lgt[:, :nt], reads=[rlgt], writes=[rlog])
            hr, rhr = TR['hrR'].next()
            for s in range(nt // 128):
                for half in range(2):
                    pt, rp = PS.next()
                    ptb = pt[:].bitcast(BF16)
                    for j in range(4):
                        c_ = half * 4 + j
                        fw.op("pe", lambda c_=c_, j=j: PE.transpose(ptb[:, j * 128:(j + 1) * 128],
                                                                    hm[:, c_, s * 128:(s + 1) * 128], ident_b[:]),
                              reads=[rhm, rC], writes=[rp])
                    fw.op("dve" if half == 0 else "pool" if False else "dve",
                          lambda half=half, s=s: V.tensor_copy(out=hr[:, s, half * 512:(half + 1) * 512], in_=ptb[:, 0:512]),
                          reads=[rp], writes=[rhr])
            ld(HM[t0:t0 + nt, 0:1024].rearrange("(s p) n -> p s n", p=128), hr[:, :nt // 128, :], reads=[rhr], writes=[rHM])

        def layer_vectors(l, b_out_ap):
            for col in range(2):
                fw.op("dve", lambda col=col: V.tensor_tensor(out=dv[:, 0, :, col], in0=modv[:, l, 2, :, col],
                                                            in1=b_out_ap, op=ALU.mult), reads=[self.rmod, rC], writes=[rdv])
                fw.op("dve", lambda col=col: V.tensor_tensor(out=dv[:, 1, :, col], in0=modv[:, l, 4, :, col],
                                                            in1=lng[:, l, 0, :], op=ALU.mult), reads=[self.rmod, rC], writes=[rdv])
                fw.op("dve", lambda col=col: V.tensor_tensor(out=dv[:, 2, :, col], in0=modv[:, l, 4, :, col],
                                                            in1=lnb[:, l, 0, :], op=ALU.mult), reads=[self.rmod, rC], writes=[rdv])
                fw.op("dve", lambda col=col: V.tensor_tensor(out=dv[:, 2, :, col], in0=dv[:, 2, :, col],
                                                            in1=modv[:, l, 3, :, col], op=ALU.add), reads=[self.rmod, rdv], writes=[rdv])

        self.breg = nc.gpsimd.to_reg(self.NS - 1)
        XSRC = [I["xT"]]

        def load_x(t0, nt):
            xin, rxin = TR['xinR'].next()
            ld(xin[:, :, :nt], XSRC[0][:, :, t0:t0 + nt], reads=[rXR], writes=[rxin])
            return xin, rxin


        def mod1(l, xin, rxin, nt, isctx):
            col = 1 if isctx else 0
            hT, rh = TR['hTR'].next()
            for c_ in range(KC):
                fw.op("act", lambda c_=c_: A.activation(out=hT[:, c_, :nt], in_=xin[:, c_, :nt], func=AF.Identity,
                                                       bias=mv(l, 0, c_, col), scale=mv(l, 1, c_, col)),
                      reads=[rxin, self.rmod], writes=[rh])
            return hT, rh

        def load_x2(t0, nt):
            xin, rxin = TR['xinR'].next()
            ld(xin[:, :, :nt], XR[:, :, t0:t0 + nt], reads=[rXR], writes=[rxin])
            return xin, rxin

        rmod_ = self.rmod
        wout = sb("wout", [128, KC, D], BF16); rwout = Res("wout")
        bo = sb("bo", [128, KC]);

        for l in range(DEPTH):
            j = l // 2
            ldc(rw_b[:], I["r_w"][l].rearrange("(c p) n -> p c n", p=128), writes=[rrw])
            if l % 2 == 0:
                self.even_layer(l, j, locals())
            else:
                self.odd_layer(l, j, locals())
            self.moe_layer(l, locals())
            XSRC[0] = XR

        with Phase(fw) as fin:
            alloc_tail(fin)
            for (t0, nt, isctx) in self.blocks:
                if isctx:
                    continue
                xin, rxin = load_x(t0, nt)
                ld(OUT[:, :, t0:t0 + nt], xin[:, :, :nt], reads=[rxin])
        fw.drain("sp")
        es.close()
        fw.close()
        return nc


class _NS:
    def __init__(self, d):
        self.__dict__.update(d)


def bc(ap, shape):
    return ap.unsqueeze(2).to_broadcast(list(shape))


def even_layer(self, l, j, Ld):
    L = _NS(Ld)
    nc, fw, I, PS, V, A, PE, G = L.nc, L.fw, L.I, L.PS, L.V, L.A, L.PE, L.G
    mm_group, ld, ldc, rC = L.mm_group, L.ld, L.ldc, L.rC
    SEQ, T, NT = self.SEQ, self.T, self.NT
    with Phase(fw) as es:
        sb = lambda name, shape, dt=F32: es.enter_context(nc.sbuf_tensor(_u(name), list(shape), dt))
        KT = sb("KT", [128, NKV, T], BF16); rKT = Res("KT")
        fw.op("pool", lambda: G.memset(KT[HD:128, :, :], 0.0), writes=[rKT])
        Vt = sb("Vt", [128, NT, NKV, HD + 1], BF16); rVt = Res("Vt")
        rP = Res("eparams")
        ldc(L.wout[:], I["out_w"][j].rearrange("(c p) n -> p c n", p=128), writes=[L.rwout])
        ld(L.bo[:], I["out_b"][:, j, :], reads=[L.rdv], writes=[rC])
        L.layer_vectors(l, L.bo[:])
        fw.op("pool", lambda: G.memset(Vt[:, :, :, HD:HD + 1], 1.0), writes=[rVt])
        if l == 0:
            zb = L.zeros[:].bitcast(BF16)
            for (a0, n0) in ((0, 16), (16 + SEQ, 32), (SEQ + 48 + CTX, 16)):
                ld(L.UT[:, :, a0:a0 + n0], zb[:, 0:4 * n0].rearrange("p (c n) -> p c n", c=4), reads=[rC], writes=[L.rUT])
        ucol = lambda t0, isctx: (SEQ + 48 + (t0 - SEQ)) if isctx else (16 + t0)

        with Phase(fw) as e1:
            L.alloc_tail(e1, lite=True)
            sb1 = lambda name, shape, dt=F32: e1.enter_context(nc.sbuf_tensor(_u(name), list(shape), dt))
            w_in = sb1("w_in", [128, KC, 1792], BF16); rwin = Res("w_in")
            btm = sb1("btm", [128, 768]); bfm = sb1("bfm", [128, 8]); qkg = sb1("qkg", [128, 10, HD])
            ldc(w_in[:], I["in_w"][j].rearrange("(c p) n -> p c n", p=128), writes=[rwin])
            ld(btm[:], I["in_b_tm"][:, j, :], writes=[rP]); ld(bfm[:], I["in_b_fm"][:, j, :], writes=[rP])
            ld(qkg[:], I["qk_g"][:, j, :].rearrange("p (h d) -> p h d", d=HD), writes=[rP])
            ropR = Ring(nc, e1, "rop", [128, 2, 32], F32, 2)
            agR = Ring(nc, e1, "ag", [128, 256], F32, 2)
            sgR = Ring(nc, e1, "sgm", [128, 256], F32, 2)
            uR = Ring(nc, e1, "uT", [128, 4, 256], BF16, 2)
            qkR = Ring(nc, e1, "qk", [128, 10, HD], F32, 2)
            sqR2 = Ring(nc, e1, "qsq", [128, 10, HD], F32, 2)
            ssR = Ring(nc, e1, "ss", [128, 10], F32, 2)
            rtR = Ring(nc, e1, "rt", [128, 10, 32], F32, 4)
            qbR = Ring(nc, e1, "qb", [128, 10, HD], BF16, 2)
            qtR = Ring(nc, e1, "qTs", [HD, NH, 256], BF16, 2)
            for (t0, nt, isctx) in self.blocks:
                xin, rxin = L.load_x(t0, nt)
                hT, rh = L.mod1(l, xin, rxin, nt, isctx)
                uT, ru = uR.next()
                for cc in range(4):
                    pa, rpa = PS.next()
                    mm_group(pa[:, :nt], [(w_in[:, kc, 768 + cc * 128:768 + (cc + 1) * 128], hT[:, kc, :nt]) for kc in range(KC)],
                             reads=[rwin, rh], writes=[rpa])
                    pg, rpg = PS.next()
                    mm_group(pg[:, :nt], [(w_in[:, kc, 1280 + cc * 128:1280 + (cc + 1) * 128], hT[:, kc, :nt]) for kc in range(KC)],
                             reads=[rwin, rh], writes=[rpg])
                    at, rat = agR.next()
                    sg, rsg = sgR.next()
                    fw.op("act", lambda: A.activation(out=at[:, :nt], in_=pa[:, :nt], func=AF.Identity, bias=bfm[:, cc:cc + 1]),
                          reads=[rpa, rP], writes=[rat])
                    fw.op("act", lambda: A.activation(out=sg[:, :nt], in_=pg[:, :nt], func=AF.Sigmoid, bias=bfm[:, 4 + cc:5 + cc]),
                          reads=[rpg, rP], writes=[rsg])
                    fw.op("dve", lambda: V.tensor_tensor(out=uT[:, cc, :nt], in0=at[:, :nt], in1=sg[:, :nt], op=ALU.mult),
                          reads=[rat, rsg], writes=[ru])
                uc = ucol(t0, isctx)
                ld(L.UT[:, :, uc:uc + nt], uT[:, :, :nt], reads=[ru], writes=[L.rUT])
                qTs, rqT = qtR.next()
                for s in range(nt // 128):
                    ti = (t0 + s * 128) // 128
                    pq, rpq = PS.next()
                    mm_group(pq[:, :], [(hT[:, kc, s * 128:(s + 1) * 128], w_in[:, kc, 0:512]) for kc in range(KC)],
                             reads=[rwin, rh], writes=[rpq])
                    pk, rpk = PS.next()
                    mm_group(pk[:, 0:256], [(hT[:, kc, s * 128:(s + 1) * 128], w_in[:, kc, 512:768]) for kc in range(KC)],
                             reads=[rwin, rh], writes=[rpk])
                    qk, rqk = qkR.next()
                    fw.op("dve", lambda: V.tensor_tensor(out=qk[:, 0:8, :], in0=pq[:, :].rearrange("p (h d) -> p h d", d=HD),
                                                         in1=btm[:, 0:512].rearrange("p (h d) -> p h d", d=HD), op=ALU.add),
                          reads=[rpq, rP], writes=[rqk])
                    fw.op("dve", lambda: V.tensor_tensor(out=qk[:, 8:10, :], in0=pk[:, 0:128].rearrange("p (h d) -> p h d", d=HD),
                                                         in1=btm[:, 512:640].rearrange("p (h d) -> p h d", d=HD), op=ALU.add),
                          reads=[rpk, rP], writes=[rqk])
                    fw.op("dve", lambda: V.tensor_tensor(out=Vt[:, ti, :, 0:HD], in0=pk[:, 128:256].rearrange("p (h d) -> p h d", d=HD),
                                                         in1=btm[:, 640:768].rearrange("p (h d) -> p h d", d=HD), op=ALU.add),
                          reads=[rpk, rP], writes=[rVt])
                    sq, rsq = sqR2.next()
                    ss, rss = ssR.next()
                    fw.op("act", lambda: A.activation(out=sq[:], in_=qk[:], func=AF.Square), reads=[rqk], writes=[rsq])
                    fw.op("dve", lambda: V.reduce_sum(out=ss[:], in_=sq[:], axis=AX.X), reads=[rsq], writes=[rss])
                    fw.op("dve", lambda: V.tensor_scalar(out=ss[:], in0=ss[:], scalar1=1.0 / HD, scalar2=EPS, op0=ALU.mult, op1=ALU.add),
                          reads=[rss], writes=[rss])
                    fw.op("act", lambda: A.activation(out=ss[:], in_=ss[:], func=AF.Sqrt), reads=[rss], writes=[rss])
                    fw.op("dve", lambda: V.reciprocal(out=ss[:], in_=ss[:]), reads=[rss], writes=[rss])
                    fw.op("dve", lambda: V.tensor_tensor(out=qk[:], in0=qk[:], in1=bc(ss[:], [128, 10, HD]), op=ALU.mult),
                          reads=[rqk, rss], writes=[rqk])
                    qb, rqb = qbR.next()
                    if isctx:
                        fw.op("dve", lambda: V.tensor_tensor(out=qb[:], in0=qk[:], in1=qkg[:], op=ALU.mult),
                              reads=[rqk, rP], writes=[rqb])
                    else:
                        fw.op("dve", lambda: V.tensor_tensor(out=qk[:], in0=qk[:], in1=qkg[:], op=ALU.mult),
                              reads=[rqk, rP], writes=[rqk])
                        x0 = qk[:].rearrange("p h (i two) -> p h i two", two=2)[:, :, :, 0]
                        x1 = qk[:].rearrange("p h (i two) -> p h i two", two=2)[:, :, :, 1]
                        o0 = qb[:].rearrange("p h (i two) -> p h i two", two=2)[:, :, :, 0]
                        o1 = qb[:].rearrange("p h (i two) -> p h i two", two=2)[:, :, :, 1]
                        rop, rrop = ropR.next()
                        ld(rop[:, 0, :], I["rope_c"][ti * 128:(ti + 1) * 128, :], writes=[rrop])
                        ld(rop[:, 1, :], I["rope_s"][ti * 128:(ti + 1) * 128, :], writes=[rrop])
                        cb_ = rop[:, 0, :].unsqueeze(1).to_broadcast([128, 10, 32])
                        sb_ = rop[:, 1, :].unsqueeze(1).to_broadcast([128, 10, 32])
                        (t1, r1), (t2, r2), (t3, r3), (t4, r4) = rtR.next(), rtR.next(), rtR.next(), rtR.next()
                        fw.op("dve", lambda: V.tensor_tensor(out=t1[:], in0=x0, in1=cb_, op=ALU.mult), reads=[rqk, rrop], writes=[r1])
                        fw.op("pool", lambda: G.tensor_tensor(out=t2[:], in0=x1, in1=sb_, op=ALU.mult), reads=[rqk, rrop], writes=[r2])
                        fw.op("dve", lambda: V.tensor_tensor(out=t3[:], in0=x0, in1=sb_, op=ALU.mult), reads=[rqk, rrop], writes=[r3])
                        fw.op("pool", lambda: G.tensor_tensor(out=t4[:], in0=x1, in1=cb_, op=ALU.mult), reads=[rqk, rrop], writes=[r4])
                        fw.op("dve", lambda: V.tensor_tensor(out=o0, in0=t1[:], in1=t2[:], op=ALU.subtract), reads=[r1, r2], writes=[rqb])
                        fw.op("dve", lambda: V.tensor_tensor(out=o1, in0=t3[:], in1=t4[:], op=ALU.add), reads=[r3, r4], writes=[rqb])
                    pt, rpt = PS.next()
                    ptb = pt[:].bitcast(BF16)
                    for hh in range(NH):
                        fw.op("pe", lambda hh=hh: PE.transpose(ptb[:HD, hh * 128:(hh + 1) * 128], qb[:, hh, :], L.ident_b[:]),
                              reads=[rqb, rC], writes=[rpt])
                    fw.op("dve", lambda: V.tensor_copy(out=qTs[:, :, s * 128:(s + 1) * 128],
                                                       in_=ptb[:HD, 0:1024].rearrange("p (h t) -> p h t", t=128)),
                          reads=[rpt], writes=[rqT])
                    pt2, rpt2 = PS.next()
                    ptb2 = pt2[:].bitcast(BF16)
                    for kk in range(NKV):
                        fw.op("pe", lambda kk=kk: PE.transpose(ptb2[:HD, kk * 128:(kk + 1) * 128], qb[:, 8 + kk, :], L.ident_b[:]),
                              reads=[rqb, rC], writes=[rpt2])
                    fw.op("dve", lambda: V.tensor_copy(out=KT[:HD, :, ti * 128:(ti + 1) * 128],
                                                       in_=ptb2[:HD, 0:256].rearrange("p (h t) -> p h t", t=128)),
                          reads=[rpt2], writes=[rKT])
                ld(L.QT[:, :, t0:t0 + nt], qTs[:, :, :nt], reads=[rqT], writes=[L.rQT])

        with Phase(fw) as e2:
            L.alloc_tail(e2)
            sb2 = lambda name, shape, dt=F32: e2.enter_context(nc.sbuf_tensor(_u(name), list(shape), dt))
            cvw = sb2("cvw", [128, 4, CW]); cvp = sb2("cvp", [128, 3, 4])
            dg = sb2("dg", [128, 4, CW, 128], BF16); rdg = Res("dg")
            ld(cvw[:], I["conv_w"][:, j], writes=[rP]); ld(cvp[:], I["conv_p"][:, j], writes=[rP])
            for cc in range(4):
                for k in range(CW):
                    fw.op("dve" if (k % 2) else "pool",
                          lambda cc=cc, k=k: (V if (k % 2) else G).tensor_scalar(
                              out=dg[:, cc, k, :], in0=L.ident_f[:], scalar1=cvw[:, cc, k:k + 1], scalar2=None, op0=ALU.mult),
                          reads=[rP, rC], writes=[rdg])
            qR = Ring(nc, e2, "Qb", [128, NH, 256], BF16, 2)
            for (qt_, rq_) in qR.items:
                fw.op("pool", lambda qt_=qt_: G.memset(qt_[HD:128, :, :], 0.0), writes=[rq_])
            pR = Ring(nc, e2, "PT", [128, 256], BF16, 6)
            atR = Ring(nc, e2, "att", [128, 2, 512], BF16, 2)
            rcR = Ring(nc, e2, "rec", [128, 2], F32, 2)
            mxR = Ring(nc, e2, "mixT", [128, KC, 256], BF16, 2)
            ubR = Ring(nc, e2, "ub", [128, 4, 256 + CW - 1], BF16, 2)
            PSA = PS.view(0, 3)
            PS.window(3, 6)
            co = None
            itc = [0]
            for (t0, nt, isctx) in self.blocks:
                ns = nt // 128
                xin, rxin = L.load_x(t0, nt)
                Qb, rQ = qR.next()
                ld(Qb[:HD, :, :nt], L.QT[:, :, t0:t0 + nt], reads=[L.rQT], writes=[rQ])
                ub, rub = ubR.next()
                uc = (SEQ + 48 + (t0 - SEQ)) if isctx else (16 + t0)
                ld(ub[:, :, :nt + CW - 1], L.UT[:, :, uc - 15:uc + nt + 15], reads=[L.rUT], writes=[rub])
                kts = list(range(SEQ // 128, NT)) if isctx else list(range(NT))
                att, ratt = atR.next()
                for h in range(NH):
                    kv = h // (NH // NKV)
                    po, rpo = L.PO.next()
                    pend = []

                    def emit_pv(item):
                        pT, rpT, ki, kt = item
                        for s in range(ns):
                            f = lambda s=s: PE.matmul(po[:, s * 65:(s + 1) * 65], lhsT=pT[:, s * 128:(s + 1) * 128],
                                                      rhs=Vt[:, kt, kv, :], start=(ki == 0 and s == 0),
                                                      stop=(ki == len(kts) - 1), skip_group_check=True)
                            if s == ns - 1:
                                fw.op("pe", f, reads=[rpT, rVt], writes=[rpo])
                            else:
                                fw.op_nt("pe", f, reads=[rpT, rVt], writes=[rpo])

                    for ki, kt in enumerate(kts):
                        itc[0] += 1
                        if co is not None and itc[0] % 3 == 0:
                            co.step()
                        pS, rpS = PSA.next()
                        fw.op("pe", lambda: PE.matmul(pS[:, :nt], lhsT=KT[:, kv, kt * 128:(kt + 1) * 128], rhs=Qb[:, h, :nt],
                                                      start=True, stop=True), reads=[rKT, rQ], writes=[rpS])
                        pT, rpT = pR.next()
                        fw.op("act", lambda: A.activation(out=pT[:, :nt], in_=pS[:, :nt], func=AF.Exp, scale=HD ** -0.5),
                              reads=[rpS], writes=[rpT])
                        pend.append((pT, rpT, ki, kt))
                        if len(pend) > 2:
                            emit_pv(pend.pop(0))
                    while pend:
                        emit_pv(pend.pop(0))
                    rec, rrec = rcR.next()
                    pov = po[:, 0:ns * 65].rearrange("p (s e) -> p s e", e=65)
                    fw.op("dve", lambda: V.reciprocal(out=rec[:, :ns], in_=pov[:, :, 64]), reads=[rpo], writes=[rrec])
                    fw.op("dve", lambda: V.tensor_tensor(out=att[:, :ns, h * HD:(h + 1) * HD], in0=pov[:, :, 0:HD],
                                                         in1=bc(rec[:, :ns], [128, ns, HD]), op=ALU.mult),
                          reads=[rpo, rrec], writes=[ratt])
                if co is not None:
                    co.drain()

                def tail_fn(t0=t0, nt=nt, isctx=isctx, ns=ns, xin=xin, rxin=rxin, att=att, ratt=ratt, ub=ub, rub=rub):
                    mixT, rmx = mxR.next()
                    for s in range(ns):
                        pt, rpt = PS.next()
                        ptb = pt[:].bitcast(BF16)
                        for c_ in range(4):
                            fw.op("pe", lambda c_=c_: PE.transpose(ptb[:, c_ * 128:(c_ + 1) * 128], att[:, s, c_ * 128:(c_ + 1) * 128],
                                                                   L.ident_b[:]), reads=[ratt, rC], writes=[rpt])
                        fw.op("dve", lambda: V.tensor_copy(out=mixT[:, 0:4, s * 128:(s + 1) * 128],
                                                           in_=ptb[:, 0:512].rearrange("p (c t) -> p c t", t=128)),
                              reads=[rpt], writes=[rmx])
                    cvt, rcv = L.TR['zR'].next()
                    for cc in range(4):
                        pc, rpc = PS.next()
                        mm_group(pc[:, :nt], [(dg[:, cc, k, :], ub[:, cc, k:k + nt]) for k in range(CW)], reads=[rdg, rub], writes=[rpc])
                        fw.op("act", lambda: A.activation(out=cvt[:, cc, :nt], in_=pc[:, :nt], func=AF.Identity, bias=cvp[:, 0, cc:cc + 1]),
                              reads=[rpc, rP], writes=[rcv])
                    L.ln_fm(cvt, rcv, 4, nt, L.ones_c)
                    for cc in range(4):
                        fw.op("act", lambda cc=cc: A.activation(out=mixT[:, 4 + cc, :nt], in_=cvt[:, cc, :nt], func=AF.Silu,
                                                               bias=cvp[:, 2, cc:cc + 1], scale=cvp[:, 1, cc:cc + 1]),
                              reads=[rcv, rP], writes=[rmx])

                    def ych(dc):
                        py, rpy = PS.next()
                        mm_group(py[:, :nt], [(L.wout[:, kc, dc * 128:(dc + 1) * 128], mixT[:, kc, :nt]) for kc in range(KC)],
                                 reads=[L.rwout, rmx], writes=[rpy])
                        return py[:, :nt], rpy
                    L.mixer_tail(l, ych, t0, nt, isctx, xin, rxin)
                co = Co(fw, tail_fn)
            if co is not None:
                co.drain()
            PS.window(0, 6)


K.even_layer = even_layer


def odd_layer(self, l, j, Ld):
    L = _NS(Ld)
    nc, fw, I, PS, V, A, PE, G = L.nc, L.fw, L.I, L.PS, L.V, L.A, L.PE, L.G
    mm_group, ld, ldc, rC = L.mm_group, L.ld, L.ldc, L.rC
    SEQ, T, NT, B = self.SEQ, self.T, self.NT, self.B
    Z, FN = L.Z, L.FN
    ldc(L.wout[:], I["fo_w"][j].rearrange("(c p) n -> p c n", p=128), writes=[L.rwout])
    ld(L.bo[:], I["fo_b"][:, j, :], reads=[L.rdv], writes=[rC])
    L.layer_vectors(l, L.bo[:])
    cp = [0]

    def evac(out, in_, reads, writes):
        cp[0] += 1
        if cp[0] % 2:
            fw.op("dve", lambda: V.tensor_copy(out=out, in_=in_), reads=reads, writes=writes)
        else:
            fw.op("act", lambda: A.copy(out=out, in_=in_), reads=reads, writes=writes)

    with Phase(fw) as o1:
        L.alloc_tail(o1, lite=True, depth=2)
        wc = o1.enter_context(nc.sbuf_tensor(_u("wc"), [128, 2, 512], BF16)); rwc = Res("wc")
        ld(wc[:], I["wc"], writes=[rwc])
        zR_ = Ring(nc, o1, "zrow", [128, 2, 2048], BF16, 2)
        for (t0, nt, isctx) in self.blocks:
            xin, rxin = L.load_x(t0, nt)
            hT, rh = L.mod1(l, xin, rxin, nt, isctx)
            zt, rzt = zR_.next()
            for s in range(nt // 128):
                for g in range(NG):
                    pz, rpz = PS.next()
                    mm_group(pz[:, :], [(hT[:, 2 * g + k2, s * 128:(s + 1) * 128], wc[:, k2, :]) for k2 in range(2)],
                             reads=[rh, rwc], writes=[rpz])
                    evac(zt[:, s, g * 512:(g + 1) * 512], pz[:, :], [rpz], [rzt])
            ld(Z[t0:t0 + nt, :].rearrange("(s p) n -> p s n", p=128), zt[:, :nt // 128, :], reads=[rzt], writes=[L.rZ])
    with Phase(fw) as o2:
        sb = lambda name, shape, dt=BF16: o2.enter_context(nc.sbuf_tensor(_u(name), list(shape), dt))
        t1, t2 = sb("t1", [128, 256]), sb("t2", [128, 256])
        mcos, msin = sb("mcos", [B, 128, B]), sb("msin", [B, 128, B])
        c256, s256 = sb("c256", [128, 2, 256]), sb("s256", [128, 2, 256])
        rT = Res("ffttab")
        for t_, nm in ((t1, "t1"), (t2, "t2"), (mcos, "mcos"), (msin, "msin"), (c256, "c256"), (s256, "s256")):
            ld(t_[:], I[nm], writes=[rT])
        zgR = Ring(nc, o2, "Zg", [128, B, 2, 128], BF16, 1)
        ybR = Ring(nc, o2, "Yb", [B, 64, 256], BF16, 2)
        fbR = Ring(nc, o2, "Fb", [B, 128, 64], BF16, 2)
        for g in range(NG):
            for cb in range(4):
                if cb % 2 == 0:
                    Zg, rZg = zgR.next()
                    hb = cb // 2
                    for ri in range(2):
                        ld(Zg[:, :, ri, :], Z[0:SEQ, g * 512 + ri * 256 + hb * 128:g * 512 + ri * 256 + (hb + 1) * 128]
                           .rearrange("(a b) n -> a b n", b=B), reads=[L.rZ], writes=[rZg])
                Yb, rYb = ybR.next()
                for i in range(32):
                    p1, rp1 = PS.next()
                    for e in range(2):
                        chl = (cb % 2) * 64 + 2 * i + e
                        mm_group(p1[:B, e * 256:(e + 1) * 256], [(Zg[:, :, 0, chl], t1[:]), (Zg[:, :, 1, chl], t2[:])],
                                 reads=[rZg, rT], writes=[rp1])
                    evac(Yb[:, 2 * i:2 * i + 2, :], p1[:B, :].rearrange("p (e c) -> p e c", c=256), [rp1], [rYb])
                Fb, rFb = fbR.next()
                for c0 in range(0, 128, 8):
                    p2, rp2 = PS.next()
                    for ci in range(8):
                        c_ = c0 + ci
                        mm_group(p2[:B, ci * 64:(ci + 1) * 64], [(mcos[:, c_, :], Yb[:, :, c_]), (msin[:, c_, :], Yb[:, :, 128 + c_])],
                                 reads=[rYb, rT], writes=[rp2])
                    evac(Fb[:, c0:c0 + 8, :], p2[:B, :].rearrange("p (c n) -> p c n", n=64), [rp2], [rFb])
                ld(FN[0:SEQ, g * 256 + cb * 64:g * 256 + (cb + 1) * 64].rearrange("(d c) n -> d c n", c=128), Fb[:],
                   reads=[rFb], writes=[L.rFN])
        Zc = sb("Zc", [128, 2, 2048]); rZc = Res("Zc")
        Fc = sb("Fc", [128, 2, 1024]); rFc = Res("Fc")
        ld(Zc[:], Z[SEQ:SEQ + CTX, :].rearrange("(c p) n -> p c n", p=128), reads=[L.rZ], writes=[rZc])
        for g in range(NG):
            for ko in range(2):
                pc, rpc = PS.next()
                pairs = []
                for n_ in range(2):
                    pairs.append((c256[:, n_, ko * 128:(ko + 1) * 128], Zc[:, n_, g * 512:g * 512 + 256]))
                    pairs.append((s256[:, n_, ko * 128:(ko + 1) * 128], Zc[:, n_, g * 512 + 256:g * 512 + 512]))
                mm_group(pc[:, 0:256], pairs, reads=[rZc, rT], writes=[rpc])
                evac(Fc[:, ko, g * 256:(g + 1) * 256], pc[:, 0:256], [rpc], [rFc])
        ld(FN[SEQ:SEQ + CTX, :].rearrange("(c p) n -> p c n", p=128), Fc[:], reads=[rFc], writes=[L.rFN])
    with Phase(fw) as o3:
        L.alloc_tail(o3, depth=2)
        frR = Ring(nc, o3, "frow", [128, 2, 1024], BF16, 2)
        ftR = Ring(nc, o3, "FT", [128, KC, 256], BF16, 2)
        for (t0, nt, isctx) in self.blocks:
            xin, rxin = L.load_x(t0, nt)
            fr, rfr = frR.next()
            ld(fr[:, :nt // 128, :], FN[t0:t0 + nt, :].rearrange("(s p) n -> p s n", p=128), reads=[L.rFN], writes=[rfr])
            FT, rFT = ftR.next()
            for s in range(nt // 128):
                pt, rpt = PS.next()
                ptb = pt[:].bitcast(BF16)
                for c_ in range(KC):
                    fw.op("pe", lambda c_=c_: PE.transpose(ptb[:, c_ * 128:(c_ + 1) * 128], fr[:, s, c_ * 128:(c_ + 1) * 128],
                                                           L.ident_b[:]), reads=[rfr, rC], writes=[rpt])
                evac(FT[:, :, s * 128:(s + 1) * 128], ptb[:, 0:1024].rearrange("p (c t) -> p c t", t=128), [rpt], [rFT])

            def ych(dc):
                py, rpy = PS.next()
                mm_group(py[:, :nt], [(L.wout[:, kc, dc * 128:(dc + 1) * 128], FT[:, kc, :nt]) for kc in range(KC)],
                         reads=[L.rwout, rFT], writes=[rpy])
                return py[:, :nt], rpy
            L.mixer_tail(l, ych, t0, nt, isctx, xin, rxin)


K.odd_layer = odd_layer


def moe_layer(self, l, Ld):
    L = _NS(Ld)
    nc, fw, I, PS, V, A, PE, G = L.nc, L.fw, L.I, L.PS, L.V, L.A, L.PE, L.G
    mm_group, ld, ldc, rC = L.mm_group, L.ld, L.ldc, L.rC
    SEQ, T, NT, FF, FC = self.SEQ, self.T, self.NT, self.FF, self.FC
    CAP, CAPC, NS, RW = self.CAP, self.CAPC, self.NS, self.RW
    NSP = ((NS + 127) // 128) * 128
    NST = NSP // 128
    HM, XG, Y, LG = L.HM, L.XG, L.Y, L.LG
    ITER = 30
    with Phase(fw) as m:
        sb = lambda name, shape, dt=F32: m.enter_context(nc.sbuf_tensor(_u(name), list(shape), dt))
        slot_i2 = sb("slot_i", [128, NT * NE], I32); rsl = Res("slot_i")
        slot_i = slot_i2[:].rearrange("p (i e) -> p i e", e=NE)
        meta = sb("meta", [128, NT, 1 + NE], I32); rme = Res("meta")
        with Phase(fw) as m1:
            sb1 = lambda name, shape, dt=F32: m1.enter_context(nc.sbuf_tensor(_u(name), list(shape), dt))
            aff = sb1("aff", [NE, T]); raf = Res("aff")
            msk = sb1("msk", [NE, T]); rmk = Res("msk")
            cs = sb1("cs", [NE, T]); rcs = Res("cs")
            stt = sb1("stt", [NE, 16]); rs_ = Res("stt")
            ld(aff[:], LG[:, :], reads=[L.rlog], writes=[raf])
            fw.op("act", lambda: A.activation(out=aff[:], in_=aff[:], func=AF.Exp), reads=[raf], writes=[raf])
            for b0 in range(0, T, 512):
                nb = min(512, T - b0)
                pp, rpp = PS.next()
                fw.op("pe", lambda: PE.matmul(pp[:NE, :nb], lhsT=L.ones16[:], rhs=aff[:, b0:b0 + nb], start=True, stop=True),
                      reads=[raf, rC], writes=[rpp])
                fw.op("dve", lambda: V.reciprocal(out=msk[:, b0:b0 + nb], in_=pp[:NE, :nb]), reads=[rpp], writes=[rmk])
            fw.op("dve", lambda: V.tensor_tensor(out=aff[:], in0=aff[:], in1=msk[:], op=ALU.mult), reads=[raf, rmk], writes=[raf])
            for c_, v in ((0, 0.0), (1, 0.0), (2, 1.5), (3, 1.5), (4, 0.75), (5, 0.75), (8, float(CAP)), (9, float(CAPC))):
                fw.op("dve", lambda c_=c_, v=v: V.memset(stt[:, c_:c_ + 1], v), writes=[rs_])
            rng = ((0, SEQ), (SEQ, T))
            for it in range(ITER):
                for q, (a0, a1) in enumerate(rng):
                    fw.op("dve", lambda q=q, a0=a0, a1=a1: V.tensor_scalar(
                        out=msk[:, a0:a1], in0=aff[:, a0:a1], scalar1=stt[:, 4 + q:5 + q], scalar2=None,
                        op0=ALU.is_ge, op1=ALU.add, accum_out=stt[:, 6 + q:7 + q]), reads=[raf, rs_], writes=[rmk, rs_])
                tt = lambda o, a, b_, op: fw.op("dve", lambda: V.tensor_tensor(out=stt[:, o:o + 2], in0=stt[:, a:a + 2],
                                                                               in1=stt[:, b_:b_ + 2], op=op), reads=[rs_], writes=[rs_])
                tt(10, 6, 8, ALU.is_ge)
                tt(12, 4, 0, ALU.subtract)
                tt(12, 12, 10, ALU.mult)
                tt(0, 0, 12, ALU.add)
                tt(12, 2, 4, ALU.subtract)
                tt(12, 12, 10, ALU.mult)
                tt(2, 4, 12, ALU.add)
                tt(4, 0, 2, ALU.add)
                fw.op("dve", lambda: V.tensor_scalar(out=stt[:, 4:6], in0=stt[:, 4:6], scalar1=0.5, scalar2=None, op0=ALU.mult),
                      reads=[rs_], writes=[rs_])
            for q, (a0, a1) in enumerate(rng):
                cap = float(CAP if q == 0 else CAPC)
                off = 0.0 if q == 0 else float(CAP)
                fw.op("dve", lambda: V.tensor_scalar(out=msk[:, a0:a1], in0=aff[:, a0:a1], scalar1=stt[:, q:q + 1], scalar2=None,
                                                     op0=ALU.is_ge), reads=[raf, rs_], writes=[rmk])
                fw.op("dve", lambda: V.tensor_tensor_scan(out=cs[:, a0:a1], data0=msk[:, a0:a1], data1=msk[:, a0:a1], initial=0.0,
                                                          op0=ALU.add, op1=ALU.max), reads=[rmk], writes=[rcs])
                fw.op("dve", lambda: V.scalar_tensor_tensor(out=msk[:, a0:a1], in0=cs[:, a0:a1], scalar=cap, in1=msk[:, a0:a1],
                                                            op0=ALU.is_le, op1=ALU.mult), reads=[rcs, rmk], writes=[rmk])
                fw.op("dve", lambda: V.scalar_tensor_tensor(out=cs[:, a0:a1], in0=cs[:, a0:a1], scalar=off - 1.0 - BIG,
                                                            in1=msk[:, a0:a1], op0=ALU.add, op1=ALU.mult), reads=[rcs, rmk], writes=[rcs])
                fw.op("dve", lambda: V.tensor_scalar(out=cs[:, a0:a1], in0=cs[:, a0:a1], scalar1=BIG, scalar2=None, op0=ALU.add),
                      reads=[rcs], writes=[rcs])
            fw.op("dve", lambda: V.tensor_copy(out=meta[:, :, 0], in_=L.tokid[:]), reads=[rC], writes=[rme])
            for g0 in range(0, NT, 16):
                gn = min(16, NT - g0)
                pt, rpt = PS.next()
                for i in range(gn):
                    ti = g0 + i
                    fw.op("pe", lambda: PE.transpose(pt[:, i * 32:i * 32 + 16], cs[:, ti * 128:(ti + 1) * 128], L.ident_f[:NE, :NE]),
                          reads=[rcs, rC], writes=[rpt])
                    fw.op("pe", lambda: PE.transpose(pt[:, i * 32 + 16:i * 32 + 32], aff[:, ti * 128:(ti + 1) * 128], L.ident_f[:NE, :NE]),
                          reads=[raf, rC], writes=[rpt])
                pv = pt[:, 0:gn * 32].rearrange("p (i k) -> p i k", k=32)
                fw.op("dve", lambda: V.tensor_copy(out=slot_i[:, g0:g0 + gn, :], in_=pv[:, :, 0:16]), reads=[rpt], writes=[rsl])
                fw.op("dve", lambda: V.tensor_copy(out=meta[:, g0:g0 + gn, 1:1 + NE].bitcast(F32), in_=pv[:, :, 16:32]),
                      reads=[rpt], writes=[rme])
            ld(HM[:, 1024:RW].rearrange("(i p) n -> p i n", p=128), meta[:].bitcast(BF16), reads=[rme], writes=[L.rHM])
        zr = L.zeros[:].rearrange("p (i n) -> p i n", n=1024)
        Yv = Y.rearrange("(i p) n -> p i n", p=128)
        for i0 in range(NT):
            ld(Yv[:, i0:i0 + 1, :], zr[:, 0:1, :], reads=[rC], writes=[L.rY])
        with Phase(fw) as m3:
            sb3 = lambda name, shape, dt=BF16: m3.enter_context(nc.sbuf_tensor(_u(name), list(shape), dt))
            xgR = Ring(nc, m3, "xg", [128, NST * RW], BF16, 1)
            XgT = sb3("XgT", [128, KC, NSP]); rXT = Res("XgT")
            hid = sb3("hid", [128, FC, NSP]); rhid = Res("hid")
            wgR = Ring(nc, m3, "wg", [128, KC, 256], BF16, 2)
            wuR = Ring(nc, m3, "wu", [128, KC, 256], BF16, 2)
            wdR = Ring(nc, m3, "wd", [128, FC, 1024], BF16, 1)
            slR = Ring(nc, m3, "sil", [128, 512], BF16, 2)
            yoR = Ring(nc, m3, "yo", [128, 1024], F32, 2)
            sblocks = [(s0, min(512, NS - s0)) for s0 in range(0, NS, 512)]
            rowR = Ring(nc, m3, "row", [128, RW], BF16, 5)
            xg_res = {}

            def dispatch_gen(e_):
                xg_res[e_] = []
                for ti in range(NT):
                    rt_, rrt = rowR.next()
                    ld(rt_[:], HM[ti * 128:(ti + 1) * 128, :], reads=[L.rHM], writes=[rrt])
                    rr_ = Res("xgs")
                    fw.dma("pool", lambda: G.indirect_dma_start(
                        out=XG[e_][:, :], out_offset=bass.IndirectOffsetOnAxis(ap=slot_i2[:, ti * NE + e_:ti * NE + e_ + 1], axis=0),
                        in_=rt_[:], in_offset=None, bounds_check=self.breg, oob_is_err=False),
                        reads=[rrt, rsl], writes=[rr_])
                    xg_res[e_].append(rr_)
                    yield

            for _ in dispatch_gen(0):
                pass
            nblk = (FC + 1) // 2
            per_blk = (NT + nblk - 1) // nblk
            for e in range(NE):
                dgen = dispatch_gen(e + 1) if e + 1 < NE else iter(())
                xg2, rxg = xgR.next()
                xg = xg2[:].rearrange("p (i n) -> p i n", n=RW)
                ld(xg, XG[e].rearrange("(i p) n -> p i n", p=128), reads=xg_res[e], writes=[rxg])
                wd, rwd0 = wdR.next()
                fw.war("pool", rwd0)
                rwd = []
                for f0 in range(0, FC, 4):
                    fn_ = min(4, FC - f0)
                    rr_ = Res("wdp")
                    ldc(wd[:, f0:f0 + fn_, :], I["w_down"][l, e, f0 * 128:(f0 + fn_) * 128, :].rearrange("(c p) n -> p c n", p=128),
                        writes=[rr_])
                    rwd.append(rr_)
                for st_ in range(NST):
                    pt, rpt = PS.next()
                    ptb = pt[:].bitcast(BF16)
                    for c_ in range(KC):
                        fw.op("pe", lambda c_=c_: PE.transpose(ptb[:, c_ * 128:(c_ + 1) * 128], xg[:, st_, c_ * 128:(c_ + 1) * 128],
                                                               L.ident_b[:]), reads=[rxg, rC], writes=[rpt])
                    fw.op("dve", lambda: V.tensor_copy(out=XgT[:, :, st_ * 128:(st_ + 1) * 128],
                                                       in_=ptb[:, 0:1024].rearrange("p (c t) -> p c t", t=128)),
                          reads=[rpt], writes=[rXT])
                for f0 in range(0, FC, 2):
                    fn_ = min(2, FC - f0)
                    wg, rwg = wgR.next()
                    wu, rwu = wuR.next()
                    ldc(wg[:, :, :fn_ * 128], I["w_gate"][l, e, :, f0 * 128:(f0 + fn_) * 128].rearrange("(c p) n -> p c n", p=128), writes=[rwg])
                    ldc(wu[:, :, :fn_ * 128], I["w_up"][l, e, :, f0 * 128:(f0 + fn_) * 128].rearrange("(c p) n -> p c n", p=128), writes=[rwu])
                    for _k in range(per_blk):
                        next(dgen, None)
                    for fi in range(fn_):
                        for (s0, sn) in sblocks:
                            pg, rpg = PS.next()
                            mm_group(pg[:, :sn], [(wg[:, kc, fi * 128:(fi + 1) * 128], XgT[:, kc, s0:s0 + sn]) for kc in range(KC)],
                                     reads=[rwg, rXT], writes=[rpg])
                            pu, rpu = PS.next()
                            mm_group(pu[:, :sn], [(wu[:, kc, fi * 128:(fi + 1) * 128], XgT[:, kc, s0:s0 + sn]) for kc in range(KC)],
                                     reads=[rwu, rXT], writes=[rpu])
                            sl, rsl_ = slR.next()
                            fw.op("act", lambda: A.activation(out=sl[:, :sn], in_=pg[:, :sn], func=AF.Silu), reads=[rpg], writes=[rsl_])
                            fw.op("dve", lambda: V.tensor_tensor(out=hid[:, f0 + fi, s0:s0 + sn], in0=sl[:, :sn], in1=pu[:, :sn], op=ALU.mult),
                                  reads=[rsl_, rpu], writes=[rhid])
                for _ in dgen:
                    pass
                for st_ in range(NST):
                    rows = min(128, NS - st_ * 128)
                    if rows <= 0:
                        continue
                    yo, ryo = yoR.next()
                    for half in range(2):
                        py, rpy = PS.next()
                        mm_group(py[:rows, :], [(hid[:, f, st_ * 128:st_ * 128 + rows], wd[:, f, half * 512:(half + 1) * 512]) for f in range(FC)],
                                 reads=[rhid, rwd0] + rwd, writes=[rpy])
                        gate = xg2[:rows, st_ * RW + 1026 + 2 * e:st_ * RW + 1028 + 2 * e].bitcast(F32)
                        fw.op("act" if half else "dve",
                              (lambda: A.activation(out=yo[:rows, 512:1024], in_=py[:rows, :], func=AF.Identity, scale=gate)) if half else
                              (lambda: V.tensor_scalar(out=yo[:rows, 0:512], in0=py[:rows, :], scalar1=gate, scalar2=None, op0=ALU.mult)),
                              reads=[rpy, rxg], writes=[ryo])
                    idx = xg2[:rows, st_ * RW + 1024:st_ * RW + 1026].bitcast(I32)
                    fw.dma("pool", lambda: G.indirect_dma_start(
                        out=Y[:, :], out_offset=bass.IndirectOffsetOnAxis(ap=idx, axis=0), in_=yo[:rows, :], in_offset=None,
                        compute_op=ALU.add), reads=[ryo, rxg], writes=[L.rY])
    with Phase(fw) as m4:
        L.alloc_tail(m4, depth=2)
        yrR = Ring(nc, m4, "yrow", [128, 2, 1024], F32, 2)
        lng, lnb, modv = L.lng, L.lnb, L.modv
        for (t0, nt, isctx) in self.blocks:
            col = 1 if isctx else 0
            xin, rxin = L.load_x2(t0, nt)
            yr, ryr = yrR.next()
            ld(yr[:, :nt // 128, :], Y[t0:t0 + nt, :].rearrange("(s p) n -> p s n", p=128), reads=[L.rY], writes=[ryr])
            zt, rz = L.TR['zR'].next()
            for s in range(nt // 128):
                for half in range(2):
                    pt, rpt = PS.next()
                    for c4 in range(4):
                        c_ = half * 4 + c4
                        fw.op("pe", lambda c_=c_, c4=c4: PE.transpose(pt[:, c4 * 128:(c4 + 1) * 128], yr[:, s, c_ * 128:(c_ + 1) * 128],
                                                                      L.ident_f[:]), reads=[ryr, rC], writes=[rpt])
                    for c4 in range(4):
                        c_ = half * 4 + c4
                        fw.op("act", lambda c_=c_, c4=c4: A.activation(out=zt[:, c_, s * 128:(s + 1) * 128], in_=pt[:, c4 * 128:(c4 + 1) * 128],
                                                                      func=AF.Identity, scale=modv[:, l, 5, c_, col:col + 1]),
                              reads=[rpt, L.rmod_], writes=[rz])
            for c_ in range(KC):
                fw.op("dve", lambda c_=c_: V.scalar_tensor_tensor(out=zt[:, c_, :nt], in0=xin[:, c_, :nt], scalar=self.alpha,
                                                                 in1=zt[:, c_, :nt], op0=ALU.mult, op1=ALU.add),
                      reads=[rxin, rz], writes=[rz])
            L.ln_fm(zt, rz, KC, nt, L.ones_d)
            xo, rxo = L.TR['xoR'].next()
            for c_ in range(KC):
                fw.op("act", lambda c_=c_: A.activation(out=xo[:, c_, :nt], in_=zt[:, c_, :nt], func=AF.Identity,
                                                       bias=lnb[:, l, 1, c_:c_ + 1], scale=lng[:, l, 1, c_:c_ + 1]),
                      reads=[rz, rC], writes=[rxo])
            ld(L.XR[:, :, t0:t0 + nt], xo[:, :, :nt], reads=[rxo], writes=[L.rXR])


K.moe_layer = moe_layer


def _fm(v):
    v = np.asarray(v, np.float32)
    lead = v.shape[:-1]
    return np.ascontiguousarray(np.moveaxis(v.reshape(lead + (KC, 128)), -1, 0))


def make_inputs(kk, b, inp):
    SEQ, DEPTH = kk.SEQ, kk.DEPTH
    NEV, NOD = (DEPTH + 1) // 2, DEPTH // 2
    f32 = lambda a: np.ascontiguousarray(np.asarray(a, np.float32))
    m = dict(kk.host_consts())
    xa = np.concatenate([inp["x"][b], inp["ctx"][b]], axis=0)
    m["xT"] = f32(xa.T.reshape(KC, 128, kk.T).transpose(1, 0, 2))
    cc = np.stack([inp["c"][b], inp["c_ctx"], inp["c"][b], inp["c_ctx"]], axis=-1)
    m["cT"] = f32(cc.reshape(KC, 128, 4).transpose(1, 0, 2))
    m["ada_w"] = f32(inp["ada_w"])
    m["ada_b"] = f32(np.asarray(inp["ada_b"]).reshape(DEPTH, 48, 128).transpose(2, 0, 1))
    m["ln_g"] = f32(_fm(inp["ln_g"]))
    m["ln_b"] = f32(_fm(inp["ln_b"]))
    m["in_w"] = f32(inp["attn_in_w"])
    ib = np.asarray(inp["attn_in_b"], np.float32)
    m["in_b_tm"] = f32(np.broadcast_to(ib[None, :, 0:768], (128, NEV, 768)))
    m["in_b_fm"] = f32(ib[:, 768:].reshape(NEV, 8, 128).transpose(2, 0, 1))
    qg = np.asarray(inp["q_norm_g"], np.float32)
    kg = np.asarray(inp["k_norm_g"], np.float32)
    qk = np.concatenate([np.tile(qg, (1, NH)), np.tile(kg, (1, NKV))], axis=1)
    m["qk_g"] = f32(np.broadcast_to(qk[None], (128, NEV, 640)))
    cw = np.asarray(inp["conv_w"], np.float32)
    m["conv_w"] = f32(cw.reshape(NEV, CW, 4, 128).transpose(3, 0, 2, 1))
    cp = np.stack([inp["conv_b"], inp["conv_ln_g"], inp["conv_ln_b"]], axis=1)
    m["conv_p"] = f32(np.asarray(cp, np.float32).reshape(NEV, 3, 4, 128).transpose(3, 0, 1, 2))
    m["out_w"] = f32(inp["attn_out_w"])
    m["out_b"] = f32(_fm(inp["attn_out_b"]))
    if NOD:
        m["fo_w"] = f32(inp["fourier_out_w"])
        m["fo_b"] = f32(_fm(inp["fourier_out_b"]))
    m["r_w"] = f32(inp["router_w"])
    m["w_gate"] = f32(inp["expert_w_gate"])
    m["w_up"] = f32(inp["expert_w_up"])
    m["w_down"] = f32(inp["expert_w_down"])
    return m


def run(inp, SEQ, FF, DEPTH, trace=False, debug=False):
    kk = K(SEQ, FF, DEPTH)
    kk.debug = debug
    nc = kk.build()
    nb = inp["x"].shape[0]
    in_maps = [make_inputs(kk, b, inp) for b in range(nb)]
    res = run_bass_kernel_spmd(nc, in_maps, core_ids=list(range(nb)), trace=trace)
    outs = []
    for b in range(nb):
        yT = res.results[b]["yT"]
        outs.append(np.ascontiguousarray(yT.transpose(2, 1, 0).reshape(SEQ, D)))
    return np.stack(outs, 0).astype(np.float32), res


def kernel(**inputs):
    out, _ = run(inputs, 8192, 2816, 4)
    return out
```
